# Optimizing a Trainium2 kernel written in Bass

```python
import math
import jax, jax.numpy as jnp
from jax import lax
import numpy as np

D_MODEL = 1024
BATCH = 4
SEQ = 8192
DEPTH = 1

D_A = D_MODEL
D_B = D_MODEL
CONV_A_WIDTH = 3
CONV_B_WIDTH = 31
D_FF = 4 * D_MODEL
N_GROUPS = 16
LN_EPS = 1e-5
ALPHA = (2.0 * DEPTH) ** 0.25
BETA = (8.0 * DEPTH) ** -0.25
W_IN_COLS = 3 * D_A + 2 * D_B + 2 * D_MODEL

kernel_name = "hybrid_shortconv_conformer_gated_deepnorm"


def layernorm(x, gamma, beta):
    xf = x.astype(jnp.float32)
    mu = jnp.mean(xf, axis=-1, keepdims=True)
    var = jnp.mean(jnp.square(xf - mu), axis=-1, keepdims=True)
    y = (xf - mu) * lax.rsqrt(var + LN_EPS)
    y = y * gamma.astype(jnp.float32) + beta.astype(jnp.float32)
    return y.astype(x.dtype)


def causal_depthwise_conv(x, w):
    k = w.shape[0]
    c = x.shape[-1]
    return lax.conv_general_dilated(
        x, w[:, None, :].astype(x.dtype),
        window_strides=(1,),
        padding=[(k - 1, 0)],
        dimension_numbers=("NWC", "WIO", "NWC"),
        feature_group_count=c,
    )


def setup_inputs(seed: int = 0) -> dict:
    key = jax.random.key(seed)
    ks = jax.random.split(key, 20)
    f32 = jnp.float32
    nrm = lambda k, shape, scale: jax.random.normal(k, shape, f32) * scale
    return {
        "x": jax.random.normal(ks[0], (BATCH, SEQ, D_MODEL), f32),
        "w_in": nrm(ks[1], (D_MODEL, W_IN_COLS), D_MODEL ** -0.5),
        "conv_a_w": nrm(ks[2], (CONV_A_WIDTH, D_A), CONV_A_WIDTH ** -0.5),
        "w_out_a": nrm(ks[3], (D_A, D_MODEL), BETA * D_A ** -0.5),
        "conv_b_w": nrm(ks[4], (CONV_B_WIDTH, D_B), CONV_B_WIDTH ** -0.5),
        "conv_b_bias": nrm(ks[5], (D_B,), 0.02),
        "ln_b_gamma": 1.0 + nrm(ks[6], (D_B,), 0.02),
        "ln_b_beta": nrm(ks[7], (D_B,), 0.02),
        "w_out_b": nrm(ks[8], (D_B, D_MODEL), BETA * D_B ** -0.5),
        "w_o": nrm(ks[9], (D_MODEL, D_MODEL), BETA * D_MODEL ** -0.5),
        "ln1_gamma": 1.0 + nrm(ks[10], (D_MODEL,), 0.02),
        "ln1_beta": nrm(ks[11], (D_MODEL,), 0.02),
        "w_up": nrm(ks[12], (D_MODEL, D_FF), D_MODEL ** -0.5),
        "w_down": nrm(ks[13], (D_FF, D_MODEL), BETA * D_FF ** -0.5),
        "ln2_gamma": 1.0 + nrm(ks[14], (D_MODEL,), 0.02),
        "ln2_beta": nrm(ks[15], (D_MODEL,), 0.02),
    }


def token_mixer(x, w_in, conv_a_w, w_out_a, conv_b_w, conv_b_bias,
                ln_b_gamma, ln_b_beta, w_out_b, w_o):
    p = jnp.einsum("bsd,dc->bsc", x, w_in)
    splits = np.cumsum([D_A, D_A, D_A, D_B, D_B, D_MODEL])
    b_a, c_a, v_a, val_b, gate_b, g_a, g_b = jnp.split(p, splits, axis=-1)

    y_a = b_a * causal_depthwise_conv(c_a * v_a, conv_a_w)
    y_a = jnp.einsum("bsc,cd->bsd", y_a, w_out_a)

    u = val_b * jax.nn.sigmoid(gate_b)
    u = causal_depthwise_conv(u, conv_b_w) + conv_b_bias.astype(u.dtype)
    u = jax.nn.silu(layernorm(u, ln_b_gamma, ln_b_beta))
    y_b = jnp.einsum("bsc,cd->bsd", u, w_out_b)

    merged = jax.nn.sigmoid(g_a) * y_a + jax.nn.sigmoid(g_b) * y_b
    return jnp.einsum("bsd,de->bse", merged, w_o)


def channel_mixer(x, w_up, w_down):
    h = jnp.square(jax.nn.relu(jnp.einsum("bsd,df->bsf", x, w_up)))
    return jnp.einsum("bsf,fd->bsd", h, w_down)


def reference(x, w_in, conv_a_w, w_out_a, conv_b_w, conv_b_bias, ln_b_gamma,
              ln_b_beta, w_out_b, w_o, ln1_gamma, ln1_beta, w_up, w_down,
              ln2_gamma, ln2_beta):
    alpha = jnp.asarray(ALPHA, dtype=x.dtype)
    for _ in range(DEPTH):
        mix = token_mixer(x, w_in, conv_a_w, w_out_a, conv_b_w, conv_b_bias,
                          ln_b_gamma, ln_b_beta, w_out_b, w_o)
        x = layernorm(alpha * x + mix, ln1_gamma, ln1_beta)
        ff = channel_mixer(x, w_up, w_down)
        x = layernorm(alpha * x + ff, ln2_gamma, ln2_beta)
    return x
```

```python
from contextlib import ExitStack

import numpy as np
import concourse.bass as bass
import concourse.mybir as mybir
from concourse.bass_utils import run_bass_kernel_spmd

F32 = mybir.dt.float32
BF16 = mybir.dt.bfloat16
AF = mybir.ActivationFunctionType
ALU = mybir.AluOpType

D = 1024
SEQ = 8192
BATCH = 4
NCORES = 8
TOK = 4096
T = 512
NPASS = TOK // T
HL = 32
KB = 31
KA = 3
NPE = 4
ALPHA = float(2.0 ** 0.25)
EPS = 1e-5
NS = 3
NMAIN = 6

B0, S30, WO0, A0, UP0, DN0 = 0, 4, 12, 14, 20, 28
NBLK = 36


class Sched:
    def __init__(self, nc, stack):
        self.nc = nc
        self.stack = stack
        self.sems = []
        self.streams = {k: [] for k in ("pe", "act", "dve", "pool", "sp")}
        self.eng_sem = {k: self.new_sem("e_" + k) for k in self.streams}
        self.eng_cnt = {k: 0 for k in self.streams}
        self.waited = {k: {} for k in self.streams}
        self.dma_cnt = {}
        self.buf_w = {}
        self.buf_r = {}

    def new_sem(self, name):
        h = self.stack.enter_context(self.nc.semaphore(name))
        self.sems.append(h)
        return len(self.sems) - 1

    def _waits(self, eng, reads, writes):
        deps = {}

        def add(tok):
            k, v = tok
            if eng == "pe" and k == self.eng_sem["pe"]:
                return
            if deps.get(k, 0) < v:
                deps[k] = v

        for b in reads:
            if b in self.buf_w:
                add(self.buf_w[b])
        for b in writes:
            if b in self.buf_w:
                add(self.buf_w[b])
            for tok in self.buf_r.get(b, {}).items():
                add(tok)
        out = []
        wd = self.waited[eng]
        for k, v in deps.items():
            if wd.get(k, 0) < v:
                wd[k] = v
                out.append((k, v))
        return out

    def _commit(self, tok, reads, writes):
        for b in reads:
            d = self.buf_r.setdefault(b, {})
            if d.get(tok[0], 0) < tok[1]:
                d[tok[0]] = tok[1]
        for b in writes:
            self.buf_w[b] = tok
            self.buf_r[b] = {}

    def op(self, eng, fn, reads=(), writes=()):
        waits = self._waits(eng, reads, writes)
        self.eng_cnt[eng] += 1
        tok = (self.eng_sem[eng], self.eng_cnt[eng])
        self.streams[eng].append((waits, fn, (tok[0], 1)))
        self._commit(tok, reads, writes)

    def dma(self, eng, semkey, out, in_, reads=(), writes=()):
        waits = self._waits(eng, reads, writes)
        self.dma_cnt[semkey] = self.dma_cnt.get(semkey, 0) + 16
        tok = (semkey, self.dma_cnt[semkey])
        self.streams[eng].append(
            (waits, (lambda e, o=out, i=in_: e.dma_start(out=o, in_=i)), (semkey, 16)))
        self._commit(tok, reads, writes)

    def final_wait(self, eng, toks):
        self.streams[eng].append((list(toks), None, None))

    def emit(self, eng, e):
        for waits, fn, inc in self.streams[eng]:
            for k, v in waits:
                e.wait_ge(self.sems[k], v)
            if fn is None:
                continue
            ins = fn(e)
            if inc is not None:
                ins.then_inc(self.sems[inc[0]], inc[1])


def build_nc():
    nc = bass.Bass("TRN2", target_bir_lowering=False)
    xblk = nc.dram_tensor("xblk", [NPASS, 128, 8, T], F32, kind="ExternalInput").ap()
    xh = nc.dram_tensor("xh", [128, 8, HL], F32, kind="ExternalInput").ap()
    wblk = nc.dram_tensor("wblk", [NBLK, 128, 4096], F32, kind="ExternalInput").ap()
    cwb_d = nc.dram_tensor("cwb", [128, 8, KB], F32, kind="ExternalInput").ap()
    cwa_d = nc.dram_tensor("cwa", [128, 8, KA], F32, kind="ExternalInput").ap()
    pv_d = nc.dram_tensor("pv", [128, 7, 8], F32, kind="ExternalInput").ap()
    ident_d = nc.dram_tensor("ident", [128, 128], F32, kind="ExternalInput").ap()
    yblk = nc.dram_tensor("yblk", [NPASS, 128, 8, T], F32, kind="ExternalOutput").ap()
    wbf = nc.dram_tensor("wbf", [NBLK, 128, 4096], BF16).ap()

    with ExitStack() as st:
        def sb(name, shape, dt):
            return st.enter_context(nc.sbuf_tensor(name, shape, dt))

        xbf = [sb(f"xbf{i}", [128, 8, T], BF16) for i in range(2)]
        xhbf = sb("xhbf", [128, 8, HL], BF16)
        tA = [sb(f"tA{i}", [128, T], F32) for i in range(4)]
        sgt = [sb(f"sgt{i}", [128, T], F32) for i in range(2)]
        hsm = [sb(f"hsm{i}", [128, HL], F32) for i in range(4)]
        cvb = [sb(f"cvb{i}", [128, T + 2], BF16) for i in range(2)]
        cvh = sb("cvh", [128, 8, 2], BF16)
        diagA = sb("diagA", [128, 8 * KA, 128], BF16)
        diagB = sb("diagB", [128, 8 * NPE, 128], BF16)
        ident = sb("ident_s", [128, 128], F32)
        apre = sb("apre", [128, 8, T], BF16)
        u0 = sb("u0", [128, 8, HL + T], BF16)
        u = sb("u", [128, 8, T], F32)
        sqt = [sb(f"sqt{i}", [128, T], BF16) for i in range(2)]
        bft = [sb(f"bft{i}", [128, T], BF16) for i in range(2)]
        sqc = [sb(f"sqc{i}", [128, T], BF16) for i in range(2)]
        bfc = [sb(f"bfc{i}", [128, T], BF16) for i in range(2)]
        mu = [sb(f"mu{i}", [128, T], F32) for i in range(2)]
        rstd = [sb(f"rstd{i}", [128, T], F32) for i in range(2)]
        tt = [sb(f"tt{i}", [128, T], F32) for i in range(2)]
        bx = sb("bx", [128, 8, T], BF16)
        res = sb("res", [128, 8, T], F32)
        h = sb("h", [128, 32, T], BF16)
        outt = [sb(f"outt{i}", [128, T], F32) for i in range(2)]
        xft = [sb(f"xft{i}", [128, T], F32) for i in range(4)]
        wsl = [sb(f"wsl{i}", [128, 4096], BF16) for i in range(NS)]
        cwb = sb("cwb_s", [128, 8, KB], F32)
        cwa = sb("cwa_s", [128, 8, KA], F32)
        pv = sb("pv_s", [128, 7, 8], F32)
        ones = sb("ones", [128, 128], BF16)
        ps = [st.enter_context(nc.psum_tensor(f"ps{i}", [128, T], F32)) for i in range(8)]

        S = Sched(nc, st)
        wsem = [S.new_sem(f"w{i}") for i in range(NS)]
        castsem = [S.new_sem(f"c{i}") for i in range(NBLK)]
        xbsem = [S.new_sem(f"xb{i}") for i in range(2)]
        xhsem = S.new_sem("xh")
        xfsem = [S.new_sem(f"xf{i}") for i in range(4)]
        osem = [S.new_sem(f"o{i}") for i in range(2)]
        csem = [S.new_sem(f"k{i}") for i in range(4)]

        seq = [B0 + k for k in range(4)] + [A0 + k for k in range(6)]
        for i in range(NPASS):
            if i < NPASS - 1:
                seq += [B0 + k for k in range(4)]
            seq += [S30 + k for k in range(8)] + [WO0, WO0 + 1]
            if i < NPASS - 1:
                seq += [A0 + k for k in range(6)]
            seq += [UP0 + k for k in range(8)] + [DN0 + k for k in range(8)]
        wstate = {"next_load": 0, "use": 0}

        def w_next(expect):
            n = wstate["use"]
            assert seq[n] == expect, (n, seq[n], expect)
            wstate["use"] += 1
            lim = min(n + NS - 1, len(seq) - 1)
            while wstate["next_load"] <= lim:
                L = wstate["next_load"]
                blk, s = seq[L], L % NS
                S.dma("sp", wsem[s], wsl[s][:], wbf[blk],
                      reads=[("wbf", blk)], writes=[("wsl", s)])
                wstate["next_load"] += 1
            return n % NS

        rot = {}

        def rt(name, n):
            v = rot.get(name, 0)
            rot[name] = (v + 1) % n
            return v

        held = set()
        busy = set()
        bank_state = {"n": 0}

        def nb():
            for _ in range(NMAIN):
                b = bank_state["n"]
                bank_state["n"] = (b + 1) % NMAIN
                if b not in held and b not in busy:
                    busy.add(b)
                    return b
            raise RuntimeError("no free PSUM bank")

        def rel(*bs):
            for b in bs:
                busy.discard(b)

        bg = []
        bgacc = {"v": 0.0, "rate": 0.0}

        def drain(n_mm):
            bgacc["v"] += n_mm * bgacc["rate"]
            if bgacc.get("busy"):
                return
            bgacc["busy"] = True
            while bg and bgacc["v"] >= 1.0:
                bgacc["v"] -= 1.0
                bg.pop(0)()
            bgacc["busy"] = False

        def flush_bg():
            bgacc["busy"] = True
            while bg:
                bg.pop(0)()
            bgacc["busy"] = False
            bgacc["v"] = 0.0

        def mm(bank_ap, pairs, reads, writes, start=True, stop=True, nd=None):
            def fn(e, pairs=pairs, bank_ap=bank_ap, start=start, stop=stop):
                last = None
                n = len(pairs)
                for idx, (l, r) in enumerate(pairs):
                    last = e.matmul(bank_ap, lhsT=l, rhs=r,
                                    start=(start and idx == 0), stop=(stop and idx == n - 1))
                return last
            S.op("pe", fn, reads=reads, writes=writes)
            drain(len(pairs) if nd is None else nd)

        def act(out, in_, func, reads, writes, bias=0.0, scale=1.0):
            S.op("act", lambda e: e.activation(out=out, in_=in_, func=func, bias=bias, scale=scale),
                 reads=reads, writes=writes)

        def tten(eng, out, in0, in1, op, reads, writes):
            S.op(eng, lambda e: e.tensor_tensor(out=out, in0=in0, in1=in1, op=op),
                 reads=reads, writes=writes)

        def tsc(eng, out, in0, s1, s2, op0, op1, reads, writes):
            if s2 is None:
                S.op(eng, lambda e: e.tensor_scalar(out=out, in0=in0, scalar1=s1, scalar2=None, op0=op0),
                     reads=reads, writes=writes)
            else:
                S.op(eng, lambda e: e.tensor_scalar(out=out, in0=in0, scalar1=s1, scalar2=s2,
                                                    op0=op0, op1=op1),
                     reads=reads, writes=writes)

        def stt(out, in0, scalar, in1, op0, op1, reads, writes):
            S.op("dve", lambda e: e.scalar_tensor_tensor(out=out, in0=in0, scalar=scalar, in1=in1,
                                                         op0=op0, op1=op1),
                 reads=reads, writes=writes)

        def copy(eng, out, in_, reads, writes):
            S.op(eng, lambda e: e.tensor_copy(out=out, in_=in_), reads=reads, writes=writes)

        def wk(s, kc, q):
            return wsl[s][:, kc * 512 + q * 128: kc * 512 + (q + 1) * 128]

        def stage_F(i, first, pre=None):
            xb = xbf[i % 2]
            s = None
            for j in range(8):
                if pre is not None:
                    ln_flush()
                    if j < 4:
                        ln_fin.append(pre(2 * j))
                        ln_fin.append(pre(2 * j + 1))
                if j % 2 == 0:
                    s = w_next(B0 + j // 2)
                q0 = (j % 2) * 2
                bv, bgk = nb(), nb()
                for which, b in ((0, bv), (1, bgk)):
                    mm(ps[b][:, :], [(wk(s, kc, q0 + which), xb[:, kc, :]) for kc in range(8)],
                       reads=[("wsl", s), ("xbf", i % 2)], writes=[("ps", b)])
                r = rt("sgt", 2)
                act(sgt[r][:], ps[bgk][:, :], AF.Tanh, reads=[("ps", bgk)], writes=[("sgt", r)], scale=0.5)
                stt(u0[:, j, HL:HL + T], sgt[r][:], 1.0, ps[bv][:, :], ALU.add, ALU.mult,
                    reads=[("ps", bv), ("sgt", r)], writes=[("u0", j)])
                rel(bv, bgk)
                if first:
                    hv, hg = nb(), nb()
                    for which, b in ((0, hv), (1, hg)):
                        mm(ps[b][:, 0:HL], [(wk(s, kc, q0 + which), xhbf[:, kc, :]) for kc in range(8)],
                           reads=[("wsl", s), "xhbf"], writes=[("ps", b)])
                    rh = rt("hsm", 4)
                    act(hsm[rh][:], ps[hg][:, 0:HL], AF.Tanh, reads=[("ps", hg)], writes=[("hsm", rh)], scale=0.5)
                    stt(u0[:, j, 0:HL], hsm[rh][:], 1.0, ps[hv][:, 0:HL], ALU.add, ALU.mult,
                        reads=[("ps", hv), ("hsm", rh)], writes=[("u0", j)])
                    rel(hv, hg)

        def conv_tasks(i):
            tasks = []

            def peconv(j):
                b = nb()
                mm(ps[b][:, :], [(diagB[:, j * NPE + k, :], u0[:, j, 2 + k:2 + k + T]) for k in range(NPE)],
                   reads=[("u0", j), "diagB"], writes=[("ps", b)], nd=0)
                act(u[:, j, :], ps[b][:, :], AF.Identity, reads=[("ps", b), "cst"], writes=[("u", j)],
                    bias=pv[:, 0, j:j + 1])
                rel(b)

            def tap(j, k):
                stt(u[:, j, :], u0[:, j, 2 + k:2 + k + T], cwb[:, j, k:k + 1], u[:, j, :],
                    ALU.mult, ALU.add, reads=[("u0", j), ("u", j), "cst"], writes=[("u", j)])

            def stats(j):
                cell = {}
                def t1():
                    r = cell["r"] = rt("sqc", 2)
                    act(sqc[r][:], u[:, j, :], AF.Square, reads=[("u", j)], writes=[("sqc", r)])
                def t2():
                    r = cell["r"]
                    act(bfc[r][:], u[:, j, :], AF.Copy, reads=[("u", j)], writes=[("bfc", r)])
                def t3():
                    r = cell["r"]
                    mm(ps[6][:, :], [(ones[:, :], bfc[r][:])], reads=[("bfc", r), "ones"],
                       writes=[("ps", 6)], start=(j == 0), stop=(j == 7), nd=0)
                def t4():
                    r = cell["r"]
                    mm(ps[7][:, :], [(ones[:, :], sqc[r][:])], reads=[("sqc", r), "ones"],
                       writes=[("ps", 7)], start=(j == 0), stop=(j == 7), nd=0)
                return [t1, t2, t3, t4]

            late = []
            for j in range(4):
                tasks.append(lambda j=j: peconv(j))
            for jp in range(4):
                ja, jb = 2 * jp, 2 * jp + 1
                for k in range(NPE, KB):
                    tasks.append(lambda j=ja, k=k: tap(j, k))
                    tasks.append(lambda j=jb, k=k: tap(j, k))
                    if k == NPE + 5:
                        tasks += late
                        late = []
                        if jp < 2:
                            tasks.append(lambda j=2 * jp + 4: peconv(j))
                            tasks.append(lambda j=2 * jp + 5: peconv(j))
                sa_, sb_ = stats(ja), stats(jb)
                tasks += sa_[:2] + sb_[:2]
                late = sa_[2:] + sb_[2:]
            tasks += late
            return tasks

        def ln_finalize(bm, be, li):
            act(mu[li][:], ps[bm][:, :], AF.Copy, reads=[("ps", bm)], writes=[("mu", li)])
            act(rstd[li][:], ps[bm][:, :], AF.Square, reads=[("ps", bm)], writes=[("rstd", li)])
            stt(rstd[li][:], ps[be][:, :], EPS, rstd[li][:], ALU.add, ALU.subtract,
                reads=[("ps", be), ("rstd", li)], writes=[("rstd", li)])
            act(rstd[li][:], rstd[li][:], AF.Ln, reads=[("rstd", li)], writes=[("rstd", li)])
            act(rstd[li][:], rstd[li][:], AF.Exp, reads=[("rstd", li)], writes=[("rstd", li)], scale=-0.5)

        def ln_center(eng, src, srcid, li):
            r = rt("tt", 2)
            tten(eng, tt[r][:], src, mu[li][:], ALU.subtract,
                 reads=[srcid, ("mu", li)], writes=[("tt", r)])
            tten(eng, tt[r][:], tt[r][:], rstd[li][:], ALU.mult,
                 reads=[("tt", r), ("rstd", li)], writes=[("tt", r)])
            return r

        def lnc_apply(j):
            r = ln_center("pool", u[:, j, :], ("u", j), 0)
            def fin():
                act(bx[:, j, :], tt[r][:], AF.Silu, reads=[("tt", r), "cst"], writes=[("bx", j)],
                    bias=pv[:, 2, j:j + 1], scale=pv[:, 1, j:j + 1])
            return fin

        def ln1_apply(e_):
            r = ln_center("pool", res[:, e_, :], ("res", e_), 0)
            def fin():
                act(res[:, e_, :], tt[r][:], AF.Identity, reads=[("tt", r), "cst"], writes=[("res", e_)],
                    bias=pv[:, 4, e_:e_ + 1], scale=pv[:, 3, e_:e_ + 1])
                act(bx[:, e_, :], tt[r][:], AF.Identity, reads=[("tt", r), "cst"], writes=[("bx", e_)],
                    bias=pv[:, 4, e_:e_ + 1], scale=pv[:, 3, e_:e_ + 1])
            return fin

        def ln2_apply(i, e_, eng="pool"):
            r = ln_center(eng, res[:, e_, :], ("res", e_), 1)
            def fin():
                ro = rt("outt", 2)
                act(outt[ro][:], tt[r][:], AF.Identity, reads=[("tt", r), "cst"], writes=[("outt", ro)],
                    bias=pv[:, 6, e_:e_ + 1], scale=pv[:, 5, e_:e_ + 1])
                S.dma("act", osem[ro], yblk[i, :, e_, :], outt[ro][:],
                      reads=[("outt", ro)], writes=[("y", i, e_)])
            return fin

        ln_fin = []

        def ln_flush(keep=0):
            while len(ln_fin) > keep:
                ln_fin.pop(0)()

        s1a_pend = []

        def stage_S1A(i, first, j):
            xb = xbf[i % 2]
            ra = rt("tA", 4)
            rb = rt("sgt", 2)
            rc = rt("cvb", 2)
            c = cvb[rc]
            cid = ("cvb", rc)
            h0 = None
            for w3 in range(3):
                q = 3 * j + w3
                if q % 4 == 0:
                    stage_S1A.s = w_next(A0 + q // 4)
                s = stage_S1A.s
                b = nb()
                mm(ps[b][:, :], [(wk(s, kc, q % 4), xb[:, kc, :]) for kc in range(8)],
                   reads=[("wsl", s), ("xbf", i % 2)], writes=[("ps", b)])
                b2 = None
                if first and w3 < 2:
                    b2 = nb()
                    mm(ps[b2][:, 0:HL], [(wk(s, kc, q % 4), xhbf[:, kc, :]) for kc in range(8)],
                       reads=[("wsl", s), "xhbf"], writes=[("ps", b2)])
                if w3 == 0:
                    act(tA[ra][:], ps[b][:, :], AF.Copy, reads=[("ps", b)], writes=[("tA", ra)])
                    rel(b)
                    if first:
                        h0 = rt("hsm", 4)
                        act(hsm[h0][:], ps[b2][:, 0:HL], AF.Copy, reads=[("ps", b2)], writes=[("hsm", h0)])
                        rel(b2)
                elif w3 == 1:
                    if first:
                        tten("dve", cvh[:, j, :], ps[b2][:, HL - 2:HL], hsm[h0][:, HL - 2:HL], ALU.mult,
                             reads=[("ps", b2), ("hsm", h0)], writes=[("cvh", j)])
                        rel(b2)
                    copy("dve", c[:, 0:2], cvh[:, j, :], reads=[("cvh", j)], writes=[cid])
                    tten("dve", c[:, 2:2 + T], ps[b][:, :], tA[ra][:], ALU.mult,
                         reads=[("ps", b), ("tA", ra)], writes=[cid])
                    rel(b)
                    copy("dve", cvh[:, j, :], c[:, T:T + 2], reads=[cid], writes=[("cvh", j)])
                else:
                    act(sgt[rb][:], ps[b][:, :], AF.Copy, reads=[("ps", b)], writes=[("sgt", rb)])
                    rel(b)
            while s1a_pend:
                s1a_pend.pop(0)()

            def tail(j=j, c=c, cid=cid, rb=rb):
                bcv = nb()
                mm(ps[bcv][:, :], [(diagA[:, j * KA + k, :], c[:, k:k + T]) for k in range(KA)],
                   reads=[cid, "diagA"], writes=[("ps", bcv)])
                tten("dve", apre[:, j, :], ps[bcv][:, :], sgt[rb][:], ALU.mult,
                     reads=[("ps", bcv), ("sgt", rb)], writes=[("apre", j)])
                rel(bcv)
            s1a_pend.append(tail)

        def stage_S3(i, extra):
            xb = xbf[i % 2]
            for j in range(8):
                s = w_next(S30 + j)
                rr = []
                for half, (rhs_t, rid) in enumerate(((apre, "apre"), (bx, "bx"))):
                    bgte, by = nb(), nb()
                    mm(ps[bgte][:, :], [(wk(s, kc, 2 * half), xb[:, kc, :]) for kc in range(8)],
                       reads=[("wsl", s), ("xbf", i % 2)], writes=[("ps", bgte)])
                    r = rt("tA", 4)
                    rr.append(r)
                    act(tA[r][:], ps[bgte][:, :], AF.Sigmoid, reads=[("ps", bgte)], writes=[("tA", r)])
                    rel(bgte)
                    mm(ps[by][:, :], [(wk(s, kc, 2 * half + 1), rhs_t[:, kc, :]) for kc in range(8)],
                       reads=[("wsl", s)] + [(rid, kc) for kc in range(8)], writes=[("ps", by)])
                    tten("dve", tA[r][:], ps[by][:, :], tA[r][:], ALU.mult,
                         reads=[("ps", by), ("tA", r)], writes=[("tA", r)])
                    rel(by)
                tten("dve" if j >= 6 else "pool", h[:, j, :], tA[rr[0]][:], tA[rr[1]][:], ALU.add,
                     reads=[("tA", rr[0]), ("tA", rr[1])], writes=[("h", j)])
                ln_flush()
                for _ in range(2):
                    if extra and j < 6:
                        ln_flush(1)
                        ln_fin.append(extra.pop(0)())
            ln_flush()
            assert not extra

        def stats_prep(e_, srcid):
            r = rt("sq", 2)
            act(sqt[r][:], res[:, e_, :], AF.Square, reads=[srcid], writes=[("sqt", r)])
            act(bft[r][:], res[:, e_, :], AF.Copy, reads=[srcid], writes=[("bft", r)])
            return r

        def stats_mm(e_, r, bm, be):
            mm(ps[bm][:, :], [(ones[:, :], bft[r][:])], reads=[("bft", r), "ones"],
               writes=[("ps", bm)], start=(e_ == 0), stop=(e_ == 7))
            mm(ps[be][:, :], [(ones[:, :], sqt[r][:])], reads=[("sqt", r), "ones"],
               writes=[("ps", be)], start=(e_ == 0), stop=(e_ == 7))

        def xload(i, e_):
            S.dma("pool", xfsem[e_ % 4], xft[e_ % 4][:], xblk[i, :, e_, :], reads=[],
                  writes=[("xft", e_ % 4)])

        def stage_mix(i):
            bm, be = nb(), nb()
            held.update((bm, be))
            pend = None
            s = w_next(WO0)
            pre = {}
            for e_ in range(3):
                pre[e_] = nb()
                mm(ps[pre[e_]][:, :], [(wk(s, kc, e_), h[:, kc, :]) for kc in range(7)],
                   reads=[("wsl", s)] + [("h", kc) for kc in range(7)], writes=[("ps", pre[e_])],
                   start=True, stop=False)
            for e_ in range(8):
                if e_ == 4:
                    s = w_next(WO0 + 1)
                if e_ in pre:
                    b = pre[e_]
                    mm(ps[b][:, :], [(wk(s, 7, e_ % 4), h[:, 7, :])],
                       reads=[("wsl", s), ("h", 7)], writes=[("ps", b)], start=False, stop=True)
                else:
                    b = nb()
                    mm(ps[b][:, :], [(wk(s, kc, e_ % 4), h[:, kc, :]) for kc in range(8)],
                       reads=[("wsl", s)] + [("h", kc) for kc in range(8)], writes=[("ps", b)])
                rx = e_ % 4
                if pend is not None:
                    stats_mm(*pend)
                stt(res[:, e_, :], xft[rx][:], ALPHA, ps[b][:, :], ALU.mult, ALU.add,
                    reads=[("xft", rx), ("ps", b)], writes=[("res", e_)])
                rel(b)
                if e_ + 4 < 8:
                    xload(i, e_ + 4)
                pend = (e_, stats_prep(e_, ("res", e_)), bm, be)
            stats_mm(*pend)
            ln_finalize(bm, be, 0)
            held.difference_update((bm, be))
            rel(bm, be)
            bank_state["n"] = (be + 1) % NMAIN

        def stage_up(i):
            s = w_next(UP0)
            pre = {}
            for f in range(3):
                pre[f] = nb()
                mm(ps[pre[f]][:, :], [(wk(s, kc, f), bx[:, kc, :]) for kc in range(7)],
                   reads=[("wsl", s)] + [("bx", kc) for kc in range(7)], writes=[("ps", pre[f])],
                   start=True, stop=False)
            for f in range(32):
                if f % 4 == 0 and f > 0:
                    s = w_next(UP0 + f // 4)
                if f in pre:
                    b = pre[f]
                    mm(ps[b][:, :], [(wk(s, 7, f % 4), bx[:, 7, :])],
                       reads=[("wsl", s), ("bx", 7)], writes=[("ps", b)], start=False, stop=True)
                else:
                    b = nb()
                    mm(ps[b][:, :], [(wk(s, kc, f % 4), bx[:, kc, :]) for kc in range(8)],
                       reads=[("wsl", s)] + [("bx", kc) for kc in range(8)], writes=[("ps", b)])
                r = rt("relu", 2)
                act(tA[r][:], ps[b][:, :], AF.Relu, reads=[("ps", b)], writes=[("tA", r)])
                rel(b)
                act(h[:, f, :], tA[r][:], AF.Square, reads=[("tA", r)], writes=[("h", f)])

        def stage_down(i):
            bm, be = nb(), nb()
            held.update((bm, be))
            pend = None
            for e_ in range(8):
                s = w_next(DN0 + e_)
                b = nb()
                dpairs = [(wsl[s][:, kc * 128:(kc + 1) * 128], h[:, kc, :]) for kc in range(32)]
                if e_ == 0:
                    mm(ps[b][:, :], dpairs[:26], reads=[("wsl", s)] + [("h", kc) for kc in range(26)],
                       writes=[("ps", b)], start=True, stop=False)
                    mm(ps[b][:, :], dpairs[26:], reads=[("wsl", s)] + [("h", kc) for kc in range(26, 32)],
                       writes=[("ps", b)], start=False, stop=True)
                else:
                    mm(ps[b][:, :], dpairs, reads=[("wsl", s)] + [("h", kc) for kc in range(32)],
                       writes=[("ps", b)])
                if pend is not None:
                    stats_mm(*pend)
                stt(res[:, e_, :], res[:, e_, :], ALPHA, ps[b][:, :], ALU.mult, ALU.add,
                    reads=[("res", e_), ("ps", b)], writes=[("res", e_)])
                rel(b)
                pend = (e_, stats_prep(e_, ("res", e_)), bm, be)
            stats_mm(*pend)
            ln_finalize(bm, be, 1)
            held.difference_update((bm, be))
            rel(bm, be)
            bank_state["n"] = (be + 1) % NMAIN

        S.dma("sp", csem[0], cwb[:], cwb_d, writes=["cst"])
        S.dma("sp", csem[1], cwa[:], cwa_d, writes=["cst"])
        S.op("dve", lambda e: e.tensor_scalar(out=cwb[:], in0=cwb[:], scalar1=0.5, scalar2=None, op0=ALU.mult),
             reads=["cst"], writes=["cst"])
        S.dma("sp", csem[2], pv[:], pv_d, writes=["cst"])
        S.dma("sp", csem[3], ident[:], ident_d, writes=["ident"])
        for j in range(8):
            for k in range(KA):
                act(diagA[:, j * KA + k, :], ident[:], AF.Copy, reads=["ident", "cst"], writes=["diagA"],
                    scale=cwa[:, j, k:k + 1])
        for j in range(8):
            for k in range(NPE):
                act(diagB[:, j * NPE + k, :], ident[:], AF.Copy, reads=["ident", "cst"], writes=["diagB"],
                    scale=cwb[:, j, k:k + 1])
        S.op("pool", lambda e: e.memset(ones[:], 1.0 / D), writes=["ones"])
        S.dma("pool", xhsem, xhbf[:], xh, writes=["xhbf"])
        S.dma("pool", xbsem[0], xbf[0][:], xblk[0], writes=[("xbf", 0)])
        first_use = []
        for b in seq:
            if b not in first_use:
                first_use.append(b)

        def casts(blks):
            for blk in blks:
                S.dma("pool", castsem[blk], wbf[blk], wblk[blk], writes=[("wbf", blk)])

        def pool_wait_loads():
            S.final_wait("pool", [(wsem[k], S.dma_cnt[wsem[k]]) for k in range(NS) if wsem[k] in S.dma_cnt])

        casts(first_use[0:6])
        S.dma("pool", xbsem[1], xbf[1][:], xblk[1], writes=[("xbf", 1)])

        stage_F(0, True)
        pool_wait_loads()
        casts(first_use[6:20])
        bg.extend(conv_tasks(0))
        bgacc["rate"] = len(bg) / 400.0
        for j in range(8):
            stage_S1A(0, True, j)
            if j == 3:
                pool_wait_loads()
                casts(first_use[20:])
        while s1a_pend:
            s1a_pend.pop(0)()
        flush_bg()

        pending_ln2 = None
        for i in range(NPASS):
            last = i == NPASS - 1
            flush_bg()
            ln_finalize(6, 7, 0)
            ln2_tasks = []
            if pending_ln2 is not None:
                ln2_tasks = [(lambda p=pending_ln2, e_=e_: ln2_apply(p, e_)) for e_ in range(8)]
                pending_ln2 = None
            if last:
                for j in range(8):
                    ln_flush()
                    ln_fin.append(lnc_apply(j))
            ln_flush()
            if not last:
                S.op("dve", lambda e: e.tensor_copy(out=u0[:, :, 0:HL], in_=u0[:, :, T:T + HL]),
                     reads=[("u0", j) for j in range(8)], writes=[("u0", j) for j in range(8)])
                bgacc["rate"] = 0.0
                stage_F(i + 1, False, pre=lnc_apply)
                ln_flush()
                bg.extend(conv_tasks(i + 1))
            nbg = len(bg)
            bgacc["rate"] = 0.22 * nbg / 280.0
            for e_ in range(4):
                xload(i, e_)
            stage_S3(i, ln2_tasks)
            if i + 2 < NPASS:
                S.dma("pool", xbsem[i % 2], xbf[i % 2][:], xblk[i + 2], writes=[("xbf", i % 2)])
            stage_mix(i)
            bgacc["rate"] = 0.20 * nbg / 280.0
            for j in range(8):
                if not last:
                    stage_S1A(i + 1, False, j)
                ln_flush()
                if j < 4:
                    ln_fin.append(ln1_apply(2 * j))
                    ln_fin.append(ln1_apply(2 * j + 1))
            ln_flush()
            while s1a_pend:
                s1a_pend.pop(0)()
            bgacc["rate"] = 0.37 * nbg / 280.0
            stage_up(i)
            bgacc["rate"] = 0.36 * nbg / 280.0
            stage_down(i)
            pending_ln2 = i
        for e_ in range(8):
            ln_flush(1)
            ln_fin.append(ln2_apply(pending_ln2, e_, "dve" if e_ % 2 == 0 else "pool"))
        ln_flush()
        S.final_wait("act", [(k, S.dma_cnt[k]) for k in osem])

        with nc.Block() as block:
            @block.tensor
            def _(e):
                S.emit("pe", e)

            @block.scalar
            def _(e):
                S.emit("act", e)

            @block.vector
            def _(e):
                S.emit("dve", e)

            @block.gpsimd
            def _(e):
                S.emit("pool", e)

            @block.sync
            def _(e):
                S.emit("sp", e)
    return nc


def _weight_blocks(w_in, w_out_a, w_out_b, w_o, w_up, w_down):
    blocks = np.empty((NBLK, 128, 4096), np.float32)

    def kblock(cols_src):
        return cols_src.reshape(8, 128, 512).transpose(1, 0, 2).reshape(128, 4096)

    cA, cC, cV, cVal, cGate, cGa, cGb = [k * D for k in range(7)]
    oB, oC, oV, oVal, oGate, oGa, oGb = cA, cC, cV, cVal, cGate, cGa, cGb

    def col(off, j):
        return w_in[:, off + j * 128: off + (j + 1) * 128]

    for m in range(4):
        cols = np.concatenate([col(oVal, 2 * m), col(oGate, 2 * m),
                               col(oVal, 2 * m + 1), col(oGate, 2 * m + 1)], axis=1)
        blocks[B0 + m] = kblock(cols)
    for j in range(8):
        cols = np.concatenate([col(oGa, j), w_out_a[:, j * 128:(j + 1) * 128],
                               col(oGb, j), w_out_b[:, j * 128:(j + 1) * 128]], axis=1)
        blocks[S30 + j] = kblock(cols)
    for m in range(2):
        blocks[WO0 + m] = kblock(w_o[:, m * 512:(m + 1) * 512])
    a_cols = []
    for j in range(8):
        a_cols += [col(oC, j), col(oV, j), col(oB, j)]
    for m in range(6):
        blocks[A0 + m] = kblock(np.concatenate(a_cols[4 * m:4 * m + 4], axis=1))
    for m in range(8):
        blocks[UP0 + m] = kblock(w_up[:, m * 512:(m + 1) * 512])
    for e_ in range(8):
        blocks[DN0 + e_] = (w_down[:, e_ * 128:(e_ + 1) * 128]
                            .reshape(32, 128, 128).transpose(1, 0, 2).reshape(128, 4096))
    return blocks


_NC_CACHE = {}


def kernel(x, w_in, conv_a_w, w_out_a, conv_b_w, conv_b_bias, ln_b_gamma, ln_b_beta,
           w_out_b, w_o, ln1_gamma, ln1_beta, w_up, w_down, ln2_gamma, ln2_beta):
    f = lambda a: np.ascontiguousarray(np.asarray(a, dtype=np.float32))
    x = f(x)
    wblk = _weight_blocks(f(w_in), f(w_out_a), f(w_out_b), f(w_o), f(w_up), f(w_down))
    cwb = np.ascontiguousarray(f(conv_b_w).T.reshape(8, 128, KB).transpose(1, 0, 2))
    cwa = np.ascontiguousarray(f(conv_a_w).T.reshape(8, 128, KA).transpose(1, 0, 2))
    vecs = [conv_b_bias, ln_b_gamma, ln_b_beta, ln1_gamma, ln1_beta, ln2_gamma, ln2_beta]
    pv = np.ascontiguousarray(np.stack([f(v).reshape(8, 128).T for v in vecs], axis=1))

    in_maps = []
    for c in range(NCORES):
        b, half = c // 2, c % 2
        t0 = half * TOK
        xs = x[b, t0:t0 + TOK, :]
        xb = np.ascontiguousarray(xs.reshape(NPASS, T, 8, 128).transpose(0, 3, 2, 1))
        if half == 0:
            xh = np.zeros((128, 8, HL), np.float32)
        else:
            xh = np.ascontiguousarray(x[b, t0 - HL:t0, :].reshape(HL, 8, 128).transpose(2, 1, 0))
        in_maps.append({"xblk": xb, "xh": xh, "wblk": wblk, "cwb": cwb, "cwa": cwa, "pv": pv,
                        "ident": np.eye(128, dtype=np.float32)})

    if "nc" not in _NC_CACHE:
        _NC_CACHE["nc"] = build_nc()
    nc = _NC_CACHE["nc"]
    res = run_bass_kernel_spmd(nc, in_maps, core_ids=list(range(NCORES)))
    out = np.empty((BATCH, SEQ, D), np.float32)
    for c in range(NCORES):
        b, half = c // 2, c % 2
        y = np.asarray(res.results[c]["yblk"], dtype=np.float32)
        out[b, half * TOK:(half + 1) * TOK, :] = y.transpose(0, 3, 2, 1).reshape(TOK, D)
    return out
```

```python
from contextlib import ExitStack

import numpy as np
import concourse.bass as bass
import concourse.mybir as mybir
from concourse.bass_utils import run_bass_kernel_spmd

F32 = mybir.dt.float32
BF16 = mybir.dt.bfloat16
AF = mybir.ActivationFunctionType
ALU = mybir.AluOpType

D = 1024
SEQ = 8192
BATCH = 4
NCORES = 8
TOK = 4096
T = 512
NPASS = TOK // T
HL = 32
KB = 31
KA = 3
NPE = 4
ALPHA = float(2.0 ** 0.25)
EPS = 1e-5
NS = 3
NMAIN = 6

B0, S30, WO0, A0, UP0, DN0 = 0, 4, 12, 14, 20, 28
NBLK = 36


class Sched:
    def __init__(self, nc, stack):
        self.nc = nc
        self.stack = stack
        self.sems = []
        self.streams = {k: [] for k in ("pe", "act", "dve", "pool", "sp")}
        self.eng_sem = {k: self.new_sem("e_" + k) for k in self.streams}
        self.eng_cnt = {k: 0 for k in self.streams}
        self.waited = {k: {} for k in self.streams}
        self.dma_cnt = {}
        self.buf_w = {}
        self.buf_r = {}

    def new_sem(self, name):
        h = self.stack.enter_context(self.nc.semaphore(name))
        self.sems.append(h)
        return len(self.sems) - 1

    def _waits(self, eng, reads, writes):
        deps = {}

        def add(tok):
            k, v = tok
            if eng == "pe" and k == self.eng_sem["pe"]:
                return
            if deps.get(k, 0) < v:
                deps[k] = v

        for b in reads:
            if b in self.buf_w:
                add(self.buf_w[b])
        for b in writes:
            if b in self.buf_w:
                add(self.buf_w[b])
            for tok in self.buf_r.get(b, {}).items():
                add(tok)
        out = []
        wd = self.waited[eng]
        for k, v in deps.items():
            if wd.get(k, 0) < v:
                wd[k] = v
                out.append((k, v))
        return out

    def _commit(self, tok, reads, writes):
        for b in reads:
            d = self.buf_r.setdefault(b, {})
            if d.get(tok[0], 0) < tok[1]:
                d[tok[0]] = tok[1]
        for b in writes:
            self.buf_w[b] = tok
            self.buf_r[b] = {}

    def op(self, eng, fn, reads=(), writes=()):
        waits = self._waits(eng, reads, writes)
        self.eng_cnt[eng] += 1
        tok = (self.eng_sem[eng], self.eng_cnt[eng])
        self.streams[eng].append((waits, fn, (tok[0], 1)))
        self._commit(tok, reads, writes)

    def dma(self, eng, semkey, out, in_, reads=(), writes=()):
        waits = self._waits(eng, reads, writes)
        self.dma_cnt[semkey] = self.dma_cnt.get(semkey, 0) + 16
        tok = (semkey, self.dma_cnt[semkey])
        self.streams[eng].append(
            (waits, (lambda e, o=out, i=in_: e.dma_start(out=o, in_=i)), (semkey, 16)))
        self._commit(tok, reads, writes)

    def final_wait(self, eng, toks):
        self.streams[eng].append((list(toks), None, None))

    def emit(self, eng, e):
        for waits, fn, inc in self.streams[eng]:
            for k, v in waits:
                e.wait_ge(self.sems[k], v)
            if fn is None:
                continue
            ins = fn(e)
            if inc is not None:
                ins.then_inc(self.sems[inc[0]], inc[1])


def build_nc():
    nc = bass.Bass("TRN2", target_bir_lowering=False)
    xblk = nc.dram_tensor("xblk", [NPASS, 128, 8, T], F32, kind="ExternalInput").ap()
    xh = nc.dram_tensor("xh", [128, 8, HL], F32, kind="ExternalInput").ap()
    wblk = nc.dram_tensor("wblk", [NBLK, 128, 4096], F32, kind="ExternalInput").ap()
    cwb_d = nc.dram_tensor("cwb", [128, 8, KB], F32, kind="ExternalInput").ap()
    cwa_d = nc.dram_tensor("cwa", [128, 8, KA], F32, kind="ExternalInput").ap()
    pv_d = nc.dram_tensor("pv", [128, 7, 8], F32, kind="ExternalInput").ap()
    ident_d = nc.dram_tensor("ident", [128, 128], F32, kind="ExternalInput").ap()
    yblk = nc.dram_tensor("yblk", [NPASS, 128, 8, T], F32, kind="ExternalOutput").ap()
    wbf = nc.dram_tensor("wbf", [NBLK, 128, 4096], BF16).ap()

    with ExitStack() as st:
        def sb(name, shape, dt):
            return st.enter_context(nc.sbuf_tensor(name, shape, dt))

        xbf = [sb(f"xbf{i}", [128, 8, T], BF16) for i in range(2)]
        xhbf = sb("xhbf", [128, 8, HL], BF16)
        tA = [sb(f"tA{i}", [128, T], F32) for i in range(4)]
        sgt = [sb(f"sgt{i}", [128, T], F32) for i in range(2)]
        hsm = [sb(f"hsm{i}", [128, HL], F32) for i in range(4)]
        cvb = [sb(f"cvb{i}", [128, T + 2], BF16) for i in range(2)]
        cvh = sb("cvh", [128, 8, 2], BF16)
        diagA = sb("diagA", [128, 8 * KA, 128], BF16)
        diagB = sb("diagB", [128, 8 * NPE, 128], BF16)
        ident = sb("ident_s", [128, 128], F32)
        apre = sb("apre", [128, 8, T], BF16)
        u0 = sb("u0", [128, 8, HL + T], BF16)
        u = sb("u", [128, 8, T], F32)
        sqt = [sb(f"sqt{i}", [128, T], BF16) for i in range(2)]
        bft = [sb(f"bft{i}", [128, T], BF16) for i in range(2)]
        sqc = [sb(f"sqc{i}", [128, T], BF16) for i in range(2)]
        bfc = [sb(f"bfc{i}", [128, T], BF16) for i in range(2)]
        mu = [sb(f"mu{i}", [128, T], F32) for i in range(2)]
        rstd = [sb(f"rstd{i}", [128, T], F32) for i in range(2)]
        tt = [sb(f"tt{i}", [128, T], F32) for i in range(2)]
        bx = sb("bx", [128, 8, T], BF16)
        res = sb("res", [128, 8, T], F32)
        h = sb("h", [128, 32, T], BF16)
        outt = [sb(f"outt{i}", [128, T], F32) for i in range(2)]
        xft = [sb(f"xft{i}", [128, T], F32) for i in range(4)]
        wsl = [sb(f"wsl{i}", [128, 4096], BF16) for i in range(NS)]
        cwb = sb("cwb_s", [128, 8, KB], F32)
        cwa = sb("cwa_s", [128, 8, KA], F32)
        pv = sb("pv_s", [128, 7, 8], F32)
        ones = sb("ones", [128, 128], BF16)
        ps = [st.enter_context(nc.psum_tensor(f"ps{i}", [128, T], F32)) for i in range(8)]

        S = Sched(nc, st)
        wsem = [S.new_sem(f"w{i}") for i in range(NS)]
        castsem = [S.new_sem(f"c{i}") for i in range(NBLK)]
        xbsem = [S.new_sem(f"xb{i}") for i in range(2)]
        xhsem = S.new_sem("xh")
        xfsem = [S.new_sem(f"xf{i}") for i in range(4)]
        osem = [S.new_sem(f"o{i}") for i in range(2)]
        csem = [S.new_sem(f"k{i}") for i in range(4)]

        seq = [B0 + k for k in range(4)] + [A0 + k for k in range(6)]
        for i in range(NPASS):
            if i < NPASS - 1:
                seq += [B0 + k for k in range(4)]
            seq += [S30 + k for k in range(8)] + [WO0, WO0 + 1]
            if i < NPASS - 1:
                seq += [A0 + k for k in range(6)]
            seq += [UP0 + k for k in range(8)] + [DN0 + k for k in range(8)]
        wstate = {"next_load": 0, "use": 0}

        def w_next(expect):
            n = wstate["use"]
            assert seq[n] == expect, (n, seq[n], expect)
            wstate["use"] += 1
            lim = min(n + NS - 1, len(seq) - 1)
            while wstate["next_load"] <= lim:
                L = wstate["next_load"]
                blk, s = seq[L], L % NS
                S.dma("sp", wsem[s], wsl[s][:], wbf[blk],
                      reads=[("wbf", blk)], writes=[("wsl", s)])
                wstate["next_load"] += 1
            return n % NS

        rot = {}

        def rt(name, n):
            v = rot.get(name, 0)
            rot[name] = (v + 1) % n
            return v

        held = set()
        busy = set()
        bank_state = {"n": 0}

        def nb():
            for _ in range(NMAIN):
                b = bank_state["n"]
                bank_state["n"] = (b + 1) % NMAIN
                if b not in held and b not in busy:
                    busy.add(b)
                    return b
            raise RuntimeError("no free PSUM bank")

        def rel(*bs):
            for b in bs:
                busy.discard(b)

        bg = []
        bgacc = {"v": 0.0, "rate": 0.0}

        def drain(n_mm):
            bgacc["v"] += n_mm * bgacc["rate"]
            if bgacc.get("busy"):
                return
            bgacc["busy"] = True
            while bg and bgacc["v"] >= 1.0:
                bgacc["v"] -= 1.0
                bg.pop(0)()
            bgacc["busy"] = False

        def flush_bg():
            bgacc["busy"] = True
            while bg:
                bg.pop(0)()
            bgacc["busy"] = False
            bgacc["v"] = 0.0

        def mm(bank_ap, pairs, reads, writes, start=True, stop=True, nd=None):
            def fn(e, pairs=pairs, bank_ap=bank_ap, start=start, stop=stop):
                last = None
                n = len(pairs)
                for idx, (l, r) in enumerate(pairs):
                    last = e.matmul(bank_ap, lhsT=l, rhs=r,
                                    start=(start and idx == 0), stop=(stop and idx == n - 1))
                return last
            S.op("pe", fn, reads=reads, writes=writes)
            drain(len(pairs) if nd is None else nd)

        def act(out, in_, func, reads, writes, bias=0.0, scale=1.0):
            S.op("act", lambda e: e.activation(out=out, in_=in_, func=func, bias=bias, scale=scale),
                 reads=reads, writes=writes)

        def tten(eng, out, in0, in1, op, reads, writes):
            S.op(eng, lambda e: e.tensor_tensor(out=out, in0=in0, in1=in1, op=op),
                 reads=reads, writes=writes)

        def tsc(eng, out, in0, s1, s2, op0, op1, reads, writes):
            if s2 is None:
                S.op(eng, lambda e: e.tensor_scalar(out=out, in0=in0, scalar1=s1, scalar2=None, op0=op0),
                     reads=reads, writes=writes)
            else:
                S.op(eng, lambda e: e.tensor_scalar(out=out, in0=in0, scalar1=s1, scalar2=s2,
                                                    op0=op0, op1=op1),
                     reads=reads, writes=writes)

        def stt(out, in0, scalar, in1, op0, op1, reads, writes):
            S.op("dve", lambda e: e.scalar_tensor_tensor(out=out, in0=in0, scalar=scalar, in1=in1,
                                                         op0=op0, op1=op1),
                 reads=reads, writes=writes)

        def copy(eng, out, in_, reads, writes):
            S.op(eng, lambda e: e.tensor_copy(out=out, in_=in_), reads=reads, writes=writes)

        def wk(s, kc, q):
            return wsl[s][:, kc * 512 + q * 128: kc * 512 + (q + 1) * 128]

        def stage_F(i, first, pre=None):
            xb = xbf[i % 2]
            s = None
            for j in range(8):
                if pre is not None:
                    ln_flush()
                    ln_fin.append(pre(j))
                if j % 2 == 0:
                    s = w_next(B0 + j // 2)
                q0 = (j % 2) * 2
                bv, bgk = nb(), nb()
                for which, b in ((0, bv), (1, bgk)):
                    mm(ps[b][:, :], [(wk(s, kc, q0 + which), xb[:, kc, :]) for kc in range(8)],
                       reads=[("wsl", s), ("xbf", i % 2)], writes=[("ps", b)])
                r = rt("sgt", 2)
                act(sgt[r][:], ps[bgk][:, :], AF.Tanh, reads=[("ps", bgk)], writes=[("sgt", r)], scale=0.5)
                stt(u0[:, j, HL:HL + T], sgt[r][:], 1.0, ps[bv][:, :], ALU.add, ALU.mult,
                    reads=[("ps", bv), ("sgt", r)], writes=[("u0", j)])
                rel(bv, bgk)
                if first:
                    hv, hg = nb(), nb()
                    for which, b in ((0, hv), (1, hg)):
                        mm(ps[b][:, 0:HL], [(wk(s, kc, q0 + which), xhbf[:, kc, :]) for kc in range(8)],
                           reads=[("wsl", s), "xhbf"], writes=[("ps", b)])
                    rh = rt("hsm", 4)
                    act(hsm[rh][:], ps[hg][:, 0:HL], AF.Tanh, reads=[("ps", hg)], writes=[("hsm", rh)], scale=0.5)
                    stt(u0[:, j, 0:HL], hsm[rh][:], 1.0, ps[hv][:, 0:HL], ALU.add, ALU.mult,
                        reads=[("ps", hv), ("hsm", rh)], writes=[("u0", j)])
                    rel(hv, hg)

        def conv_tasks(i):
            tasks = []

            def peconv(j):
                b = nb()
                mm(ps[b][:, :], [(diagB[:, j * NPE + k, :], u0[:, j, 2 + k:2 + k + T]) for k in range(NPE)],
                   reads=[("u0", j), "diagB"], writes=[("ps", b)], nd=0)
                act(u[:, j, :], ps[b][:, :], AF.Identity, reads=[("ps", b), "cst"], writes=[("u", j)],
                    bias=pv[:, 0, j:j + 1])
                rel(b)

            def tap(j, k):
                stt(u[:, j, :], u0[:, j, 2 + k:2 + k + T], cwb[:, j, k:k + 1], u[:, j, :],
                    ALU.mult, ALU.add, reads=[("u0", j), ("u", j), "cst"], writes=[("u", j)])

            def stats(j):
                cell = {}
                def t1():
                    r = cell["r"] = rt("sqc", 2)
                    act(sqc[r][:], u[:, j, :], AF.Square, reads=[("u", j)], writes=[("sqc", r)])
                def t2():
                    r = cell["r"]
                    act(bfc[r][:], u[:, j, :], AF.Copy, reads=[("u", j)], writes=[("bfc", r)])
                def t3():
                    r = cell["r"]
                    mm(ps[6][:, :], [(ones[:, :], bfc[r][:])], reads=[("bfc", r), "ones"],
                       writes=[("ps", 6)], start=(j == 0), stop=(j == 7), nd=0)
                def t4():
                    r = cell["r"]
                    mm(ps[7][:, :], [(ones[:, :], sqc[r][:])], reads=[("sqc", r), "ones"],
                       writes=[("ps", 7)], start=(j == 0), stop=(j == 7), nd=0)
                return [t1, t2, t3, t4]

            late = []
            for j in range(4):
                tasks.append(lambda j=j: peconv(j))
            for jp in range(4):
                ja, jb = 2 * jp, 2 * jp + 1
                for k in range(NPE, KB):
                    tasks.append(lambda j=ja, k=k: tap(j, k))
                    tasks.append(lambda j=jb, k=k: tap(j, k))
                    if k == NPE + 5:
                        tasks += late
                        late = []
                        if jp < 2:
                            tasks.append(lambda j=2 * jp + 4: peconv(j))
                            tasks.append(lambda j=2 * jp + 5: peconv(j))
                sa_, sb_ = stats(ja), stats(jb)
                tasks += sa_[:2] + sb_[:2]
                late = sa_[2:] + sb_[2:]
            tasks += late
            return tasks

        def ln_finalize(bm, be, li):
            act(mu[li][:], ps[bm][:, :], AF.Copy, reads=[("ps", bm)], writes=[("mu", li)])
            act(rstd[li][:], ps[bm][:, :], AF.Square, reads=[("ps", bm)], writes=[("rstd", li)])
            stt(rstd[li][:], ps[be][:, :], EPS, rstd[li][:], ALU.add, ALU.subtract,
                reads=[("ps", be), ("rstd", li)], writes=[("rstd", li)])
            act(rstd[li][:], rstd[li][:], AF.Ln, reads=[("rstd", li)], writes=[("rstd", li)])
            act(rstd[li][:], rstd[li][:], AF.Exp, reads=[("rstd", li)], writes=[("rstd", li)], scale=-0.5)

        def ln_center(eng, src, srcid, li):
            r = rt("tt", 2)
            tten(eng, tt[r][:], src, mu[li][:], ALU.subtract,
                 reads=[srcid, ("mu", li)], writes=[("tt", r)])
            tten(eng, tt[r][:], tt[r][:], rstd[li][:], ALU.mult,
                 reads=[("tt", r), ("rstd", li)], writes=[("tt", r)])
            return r

        def lnc_apply(j):
            r = ln_center("pool", u[:, j, :], ("u", j), 0)
            def fin():
                act(bx[:, j, :], tt[r][:], AF.Silu, reads=[("tt", r), "cst"], writes=[("bx", j)],
                    bias=pv[:, 2, j:j + 1], scale=pv[:, 1, j:j + 1])
            return fin

        def ln1_apply(e_):
            r = ln_center("pool", res[:, e_, :], ("res", e_), 0)
            def fin():
                act(res[:, e_, :], tt[r][:], AF.Identity, reads=[("tt", r), "cst"], writes=[("res", e_)],
                    bias=pv[:, 4, e_:e_ + 1], scale=pv[:, 3, e_:e_ + 1])
                act(bx[:, e_, :], tt[r][:], AF.Identity, reads=[("tt", r), "cst"], writes=[("bx", e_)],
                    bias=pv[:, 4, e_:e_ + 1], scale=pv[:, 3, e_:e_ + 1])
            return fin

        def ln2_apply(i, e_, eng="pool"):
            r = ln_center(eng, res[:, e_, :], ("res", e_), 1)
            def fin():
                ro = rt("outt", 2)
                act(outt[ro][:], tt[r][:], AF.Identity, reads=[("tt", r), "cst"], writes=[("outt", ro)],
                    bias=pv[:, 6, e_:e_ + 1], scale=pv[:, 5, e_:e_ + 1])
                S.dma("act", osem[ro], yblk[i, :, e_, :], outt[ro][:],
                      reads=[("outt", ro)], writes=[("y", i, e_)])
            return fin

        ln_fin = []

        def ln_flush(keep=0):
            while len(ln_fin) > keep:
                ln_fin.pop(0)()

        s1a_pend = []

        def stage_S1A(i, first, j):
            xb = xbf[i % 2]
            ra = rt("tA", 4)
            rb = rt("sgt", 2)
            rc = rt("cvb", 2)
            c = cvb[rc]
            cid = ("cvb", rc)
            h0 = None
            for w3 in range(3):
                q = 3 * j + w3
                if q % 4 == 0:
                    stage_S1A.s = w_next(A0 + q // 4)
                s = stage_S1A.s
                b = nb()
                mm(ps[b][:, :], [(wk(s, kc, q % 4), xb[:, kc, :]) for kc in range(8)],
                   reads=[("wsl", s), ("xbf", i % 2)], writes=[("ps", b)])
                b2 = None
                if first and w3 < 2:
                    b2 = nb()
                    mm(ps[b2][:, 0:HL], [(wk(s, kc, q % 4), xhbf[:, kc, :]) for kc in range(8)],
                       reads=[("wsl", s), "xhbf"], writes=[("ps", b2)])
                if w3 == 0:
                    act(tA[ra][:], ps[b][:, :], AF.Copy, reads=[("ps", b)], writes=[("tA", ra)])
                    rel(b)
                    if first:
                        h0 = rt("hsm", 4)
                        act(hsm[h0][:], ps[b2][:, 0:HL], AF.Copy, reads=[("ps", b2)], writes=[("hsm", h0)])
                        rel(b2)
                elif w3 == 1:
                    if first:
                        tten("dve", cvh[:, j, :], ps[b2][:, HL - 2:HL], hsm[h0][:, HL - 2:HL], ALU.mult,
                             reads=[("ps", b2), ("hsm", h0)], writes=[("cvh", j)])
                        rel(b2)
                    copy("dve", c[:, 0:2], cvh[:, j, :], reads=[("cvh", j)], writes=[cid])
                    tten("dve", c[:, 2:2 + T], ps[b][:, :], tA[ra][:], ALU.mult,
                         reads=[("ps", b), ("tA", ra)], writes=[cid])
                    rel(b)
                    copy("dve", cvh[:, j, :], c[:, T:T + 2], reads=[cid], writes=[("cvh", j)])
                else:
                    act(sgt[rb][:], ps[b][:, :], AF.Copy, reads=[("ps", b)], writes=[("sgt", rb)])
                    rel(b)
            while s1a_pend:
                s1a_pend.pop(0)()

            def tail(j=j, c=c, cid=cid, rb=rb):
                bcv = nb()
                mm(ps[bcv][:, :], [(diagA[:, j * KA + k, :], c[:, k:k + T]) for k in range(KA)],
                   reads=[cid, "diagA"], writes=[("ps", bcv)])
                tten("dve", apre[:, j, :], ps[bcv][:, :], sgt[rb][:], ALU.mult,
                     reads=[("ps", bcv), ("sgt", rb)], writes=[("apre", j)])
                rel(bcv)
            s1a_pend.append(tail)

        def stage_S3(i, extra):
            xb = xbf[i % 2]
            for j in range(8):
                s = w_next(S30 + j)
                rr = []
                for half, (rhs_t, rid) in enumerate(((apre, "apre"), (bx, "bx"))):
                    bgte, by = nb(), nb()
                    mm(ps[bgte][:, :], [(wk(s, kc, 2 * half), xb[:, kc, :]) for kc in range(8)],
                       reads=[("wsl", s), ("xbf", i % 2)], writes=[("ps", bgte)])
                    r = rt("tA", 4)
                    rr.append(r)
                    act(tA[r][:], ps[bgte][:, :], AF.Sigmoid, reads=[("ps", bgte)], writes=[("tA", r)])
                    rel(bgte)
                    mm(ps[by][:, :], [(wk(s, kc, 2 * half + 1), rhs_t[:, kc, :]) for kc in range(8)],
                       reads=[("wsl", s)] + [(rid, kc) for kc in range(8)], writes=[("ps", by)])
                    tten("dve", tA[r][:], ps[by][:, :], tA[r][:], ALU.mult,
                         reads=[("ps", by), ("tA", r)], writes=[("tA", r)])
                    rel(by)
                tten("dve" if j >= 6 else "pool", h[:, j, :], tA[rr[0]][:], tA[rr[1]][:], ALU.add,
                     reads=[("tA", rr[0]), ("tA", rr[1])], writes=[("h", j)])
                ln_flush()
                for _ in range(2):
                    if extra and j < 6:
                        ln_flush(1)
                        ln_fin.append(extra.pop(0)())
            ln_flush()
            assert not extra

        def stats_prep(e_, srcid):
            r = rt("sq", 2)
            act(sqt[r][:], res[:, e_, :], AF.Square, reads=[srcid], writes=[("sqt", r)])
            act(bft[r][:], res[:, e_, :], AF.Copy, reads=[srcid], writes=[("bft", r)])
            return r

        def stats_mm(e_, r, bm, be):
            mm(ps[bm][:, :], [(ones[:, :], bft[r][:])], reads=[("bft", r), "ones"],
               writes=[("ps", bm)], start=(e_ == 0), stop=(e_ == 7))
            mm(ps[be][:, :], [(ones[:, :], sqt[r][:])], reads=[("sqt", r), "ones"],
               writes=[("ps", be)], start=(e_ == 0), stop=(e_ == 7))

        def xload(i, e_):
            S.dma("pool", xfsem[e_ % 4], xft[e_ % 4][:], xblk[i, :, e_, :], reads=[],
                  writes=[("xft", e_ % 4)])

        def stage_mix(i):
            bm, be = nb(), nb()
            held.update((bm, be))
            pend = None
            s = w_next(WO0)
            pre = {}
            for e_ in range(3):
                pre[e_] = nb()
                mm(ps[pre[e_]][:, :], [(wk(s, kc, e_), h[:, kc, :]) for kc in range(7)],
                   reads=[("wsl", s)] + [("h", kc) for kc in range(7)], writes=[("ps", pre[e_])],
                   start=True, stop=False)
            for e_ in range(8):
                if e_ == 4:
                    s = w_next(WO0 + 1)
                if e_ in pre:
                    b = pre[e_]
                    mm(ps[b][:, :], [(wk(s, 7, e_ % 4), h[:, 7, :])],
                       reads=[("wsl", s), ("h", 7)], writes=[("ps", b)], start=False, stop=True)
                else:
                    b = nb()
                    mm(ps[b][:, :], [(wk(s, kc, e_ % 4), h[:, kc, :]) for kc in range(8)],
                       reads=[("wsl", s)] + [("h", kc) for kc in range(8)], writes=[("ps", b)])
                rx = e_ % 4
                if pend is not None:
                    stats_mm(*pend)
                stt(res[:, e_, :], xft[rx][:], ALPHA, ps[b][:, :], ALU.mult, ALU.add,
                    reads=[("xft", rx), ("ps", b)], writes=[("res", e_)])
                rel(b)
                if e_ + 4 < 8:
                    xload(i, e_ + 4)
                pend = (e_, stats_prep(e_, ("res", e_)), bm, be)
            stats_mm(*pend)
            ln_finalize(bm, be, 0)
            held.difference_update((bm, be))
            rel(bm, be)
            bank_state["n"] = (be + 1) % NMAIN

        def stage_up(i):
            s = w_next(UP0)
            pre = {}
            for f in range(3):
                pre[f] = nb()
                mm(ps[pre[f]][:, :], [(wk(s, kc, f), bx[:, kc, :]) for kc in range(7)],
                   reads=[("wsl", s)] + [("bx", kc) for kc in range(7)], writes=[("ps", pre[f])],
                   start=True, stop=False)
            for f in range(32):
                if f % 4 == 0 and f > 0:
                    s = w_next(UP0 + f // 4)
                if f in pre:
                    b = pre[f]
                    mm(ps[b][:, :], [(wk(s, 7, f % 4), bx[:, 7, :])],
                       reads=[("wsl", s), ("bx", 7)], writes=[("ps", b)], start=False, stop=True)
                else:
                    b = nb()
                    mm(ps[b][:, :], [(wk(s, kc, f % 4), bx[:, kc, :]) for kc in range(8)],
                       reads=[("wsl", s)] + [("bx", kc) for kc in range(8)], writes=[("ps", b)])
                r = rt("relu", 2)
                act(tA[r][:], ps[b][:, :], AF.Relu, reads=[("ps", b)], writes=[("tA", r)])
                rel(b)
                act(h[:, f, :], tA[r][:], AF.Square, reads=[("tA", r)], writes=[("h", f)])

        def stage_down(i):
            bm, be = nb(), nb()
            held.update((bm, be))
            pend = None
            for e_ in range(8):
                s = w_next(DN0 + e_)
                b = nb()
                dpairs = [(wsl[s][:, kc * 128:(kc + 1) * 128], h[:, kc, :]) for kc in range(32)]
                if e_ == 0:
                    mm(ps[b][:, :], dpairs[:26], reads=[("wsl", s)] + [("h", kc) for kc in range(26)],
                       writes=[("ps", b)], start=True, stop=False)
                    mm(ps[b][:, :], dpairs[26:], reads=[("wsl", s)] + [("h", kc) for kc in range(26, 32)],
                       writes=[("ps", b)], start=False, stop=True)
                else:
                    mm(ps[b][:, :], dpairs, reads=[("wsl", s)] + [("h", kc) for kc in range(32)],
                       writes=[("ps", b)])
                if pend is not None:
                    stats_mm(*pend)
                stt(res[:, e_, :], res[:, e_, :], ALPHA, ps[b][:, :], ALU.mult, ALU.add,
                    reads=[("res", e_), ("ps", b)], writes=[("res", e_)])
                rel(b)
                pend = (e_, stats_prep(e_, ("res", e_)), bm, be)
            stats_mm(*pend)
            ln_finalize(bm, be, 1)
            held.difference_update((bm, be))
            rel(bm, be)
            bank_state["n"] = (be + 1) % NMAIN

        S.dma("sp", csem[0], cwb[:], cwb_d, writes=["cst"])
        S.dma("sp", csem[1], cwa[:], cwa_d, writes=["cst"])
        S.op("dve", lambda e: e.tensor_scalar(out=cwb[:], in0=cwb[:], scalar1=0.5, scalar2=None, op0=ALU.mult),
             reads=["cst"], writes=["cst"])
        S.dma("sp", csem[2], pv[:], pv_d, writes=["cst"])
        S.dma("sp", csem[3], ident[:], ident_d, writes=["ident"])
        for j in range(8):
            for k in range(KA):
                act(diagA[:, j * KA + k, :], ident[:], AF.Copy, reads=["ident", "cst"], writes=["diagA"],
                    scale=cwa[:, j, k:k + 1])
        for j in range(8):
            for k in range(NPE):
                act(diagB[:, j * NPE + k, :], ident[:], AF.Copy, reads=["ident", "cst"], writes=["diagB"],
                    scale=cwb[:, j, k:k + 1])
        S.op("pool", lambda e: e.memset(ones[:], 1.0 / D), writes=["ones"])
        S.dma("pool", xhsem, xhbf[:], xh, writes=["xhbf"])
        S.dma("pool", xbsem[0], xbf[0][:], xblk[0], writes=[("xbf", 0)])
        first_use = []
        for b in seq:
            if b not in first_use:
                first_use.append(b)

        def casts(blks):
            for blk in blks:
                S.dma("pool", castsem[blk], wbf[blk], wblk[blk], writes=[("wbf", blk)])

        def pool_wait_loads():
            S.final_wait("pool", [(wsem[k], S.dma_cnt[wsem[k]]) for k in range(NS) if wsem[k] in S.dma_cnt])

        casts(first_use[0:4])
        S.dma("pool", xbsem[1], xbf[1][:], xblk[1], writes=[("xbf", 1)])
        S.final_wait("pool", [(wsem[0], 16)])
        casts(first_use[4:10])

        stage_F(0, True)
        pool_wait_loads()
        casts(first_use[10:20])
        bg.extend(conv_tasks(0))
        bgacc["rate"] = len(bg) / 400.0
        for j in range(8):
            stage_S1A(0, True, j)
            if j == 3:
                pool_wait_loads()
                casts(first_use[20:])
        while s1a_pend:
            s1a_pend.pop(0)()
        flush_bg()

        pending_ln2 = None
        for i in range(NPASS):
            last = i == NPASS - 1
            flush_bg()
            ln_finalize(6, 7, 0)
            ln2_tasks = []
            if pending_ln2 is not None:
                ln2_tasks = [(lambda p=pending_ln2, e_=e_: ln2_apply(p, e_)) for e_ in range(8)]
                pending_ln2 = None
            if last:
                for j in range(8):
                    ln_flush()
                    ln_fin.append(lnc_apply(j))
            ln_flush()
            if not last:
                S.op("dve", lambda e: e.tensor_copy(out=u0[:, :, 0:HL], in_=u0[:, :, T:T + HL]),
                     reads=[("u0", j) for j in range(8)], writes=[("u0", j) for j in range(8)])
                bgacc["rate"] = 0.0
                stage_F(i + 1, False, pre=lnc_apply)
                ln_flush()
                bg.extend(conv_tasks(i + 1))
            nbg = len(bg)
            bgacc["rate"] = 0.22 * nbg / 280.0
            for e_ in range(4):
                xload(i, e_)
            stage_S3(i, ln2_tasks)
            if i + 2 < NPASS:
                S.dma("pool", xbsem[i % 2], xbf[i % 2][:], xblk[i + 2], writes=[("xbf", i % 2)])
            stage_mix(i)
            bgacc["rate"] = 0.20 * nbg / 280.0
            for j in range(8):
                if not last:
                    stage_S1A(i + 1, False, j)
                ln_flush()
                if j < 4:
                    ln_fin.append(ln1_apply(2 * j))
                    ln_fin.append(ln1_apply(2 * j + 1))
            ln_flush()
            while s1a_pend:
                s1a_pend.pop(0)()
            bgacc["rate"] = 0.37 * nbg / 280.0
            stage_up(i)
            bgacc["rate"] = 0.36 * nbg / 280.0
            stage_down(i)
            pending_ln2 = i
        for e_ in range(8):
            ln_flush(1)
            ln_fin.append(ln2_apply(pending_ln2, e_, "pool" if e_ % 4 == 1 else "dve"))
        ln_flush()
        S.final_wait("act", [(k, S.dma_cnt[k]) for k in osem])

        with nc.Block() as block:
            @block.tensor
            def _(e):
                S.emit("pe", e)

            @block.scalar
            def _(e):
                S.emit("act", e)

            @block.vector
            def _(e):
                S.emit("dve", e)

            @block.gpsimd
            def _(e):
                S.emit("pool", e)

            @block.sync
            def _(e):
                S.emit("sp", e)
    return nc


def _weight_blocks(w_in, w_out_a, w_out_b, w_o, w_up, w_down):
    blocks = np.empty((NBLK, 128, 4096), np.float32)

    def kblock(cols_src):
        return cols_src.reshape(8, 128, 512).transpose(1, 0, 2).reshape(128, 4096)

    cA, cC, cV, cVal, cGate, cGa, cGb = [k * D for k in range(7)]
    oB, oC, oV, oVal, oGate, oGa, oGb = cA, cC, cV, cVal, cGate, cGa, cGb

    def col(off, j):
        return w_in[:, off + j * 128: off + (j + 1) * 128]

    for m in range(4):
        cols = np.concatenate([col(oVal, 2 * m), col(oGate, 2 * m),
                               col(oVal, 2 * m + 1), col(oGate, 2 * m + 1)], axis=1)
        blocks[B0 + m] = kblock(cols)
    for j in range(8):
        cols = np.concatenate([col(oGa, j), w_out_a[:, j * 128:(j + 1) * 128],
                               col(oGb, j), w_out_b[:, j * 128:(j + 1) * 128]], axis=1)
        blocks[S30 + j] = kblock(cols)
    for m in range(2):
        blocks[WO0 + m] = kblock(w_o[:, m * 512:(m + 1) * 512])
    a_cols = []
    for j in range(8):
        a_cols += [col(oC, j), col(oV, j), col(oB, j)]
    for m in range(6):
        blocks[A0 + m] = kblock(np.concatenate(a_cols[4 * m:4 * m + 4], axis=1))
    for m in range(8):
        blocks[UP0 + m] = kblock(w_up[:, m * 512:(m + 1) * 512])
    for e_ in range(8):
        blocks[DN0 + e_] = (w_down[:, e_ * 128:(e_ + 1) * 128]
                            .reshape(32, 128, 128).transpose(1, 0, 2).reshape(128, 4096))
    return blocks


_NC_CACHE = {}


def kernel(x, w_in, conv_a_w, w_out_a, conv_b_w, conv_b_bias, ln_b_gamma, ln_b_beta,
           w_out_b, w_o, ln1_gamma, ln1_beta, w_up, w_down, ln2_gamma, ln2_beta):
    f = lambda a: np.ascontiguousarray(np.asarray(a, dtype=np.float32))
    x = f(x)
    wblk = _weight_blocks(f(w_in), f(w_out_a), f(w_out_b), f(w_o), f(w_up), f(w_down))
    cwb = np.ascontiguousarray(f(conv_b_w).T.reshape(8, 128, KB).transpose(1, 0, 2))
    cwa = np.ascontiguousarray(f(conv_a_w).T.reshape(8, 128, KA).transpose(1, 0, 2))
    vecs = [conv_b_bias, ln_b_gamma, ln_b_beta, ln1_gamma, ln1_beta, ln2_gamma, ln2_beta]
    pv = np.ascontiguousarray(np.stack([f(v).reshape(8, 128).T for v in vecs], axis=1))

    in_maps = []
    for c in range(NCORES):
        b, half = c // 2, c % 2
        t0 = half * TOK
        xs = x[b, t0:t0 + TOK, :]
        xb = np.ascontiguousarray(xs.reshape(NPASS, T, 8, 128).transpose(0, 3, 2, 1))
        if half == 0:
            xh = np.zeros((128, 8, HL), np.float32)
        else:
            xh = np.ascontiguousarray(x[b, t0 - HL:t0, :].reshape(HL, 8, 128).transpose(2, 1, 0))
        in_maps.append({"xblk": xb, "xh": xh, "wblk": wblk, "cwb": cwb, "cwa": cwa, "pv": pv,
                        "ident": np.eye(128, dtype=np.float32)})

    if "nc" not in _NC_CACHE:
        _NC_CACHE["nc"] = build_nc()
    nc = _NC_CACHE["nc"]
    res = run_bass_kernel_spmd(nc, in_maps, core_ids=list(range(NCORES)))
    out = np.empty((BATCH, SEQ, D), np.float32)
    for c in range(NCORES):
        b, half = c // 2, c % 2
        y = np.asarray(res.results[c]["yblk"], dtype=np.float32)
        out[b, half * TOK:(half + 1) * TOK, :] = y.transpose(0, 3, 2, 1).reshape(TOK, D)
    return out
```

```python
from contextlib import ExitStack

import numpy as np
import concourse.bass as bass
import concourse.mybir as mybir
from concourse.bass_utils import run_bass_kernel_spmd

F32 = mybir.dt.float32
BF16 = mybir.dt.bfloat16
AF = mybir.ActivationFunctionType
ALU = mybir.AluOpType

D = 1024
SEQ = 8192
BATCH = 4
NCORES = 8
TOK = 4096
T = 512
NPASS = TOK // T
HL = 32
KB = 31
KA = 3
NPE = 4
ALPHA = float(2.0 ** 0.25)
EPS = 1e-5
NS = 3
NMAIN = 6

B0, S30, WO0, A0, UP0, DN0 = 0, 4, 12, 14, 20, 28
NBLK = 36


class Sched:
    def __init__(self, nc, stack):
        self.nc = nc
        self.stack = stack
        self.sems = []
        self.streams = {k: [] for k in ("pe", "act", "dve", "pool", "sp")}
        self.eng_sem = {k: self.new_sem("e_" + k) for k in self.streams}
        self.eng_cnt = {k: 0 for k in self.streams}
        self.waited = {k: {} for k in self.streams}
        self.dma_cnt = {}
        self.buf_w = {}
        self.buf_r = {}

    def new_sem(self, name):
        h = self.stack.enter_context(self.nc.semaphore(name))
        self.sems.append(h)
        return len(self.sems) - 1

    def _waits(self, eng, reads, writes):
        deps = {}

        def add(tok):
            k, v = tok
            if eng == "pe" and k == self.eng_sem["pe"]:
                return
            if deps.get(k, 0) < v:
                deps[k] = v

        for b in reads:
            if b in self.buf_w:
                add(self.buf_w[b])
        for b in writes:
            if b in self.buf_w:
                add(self.buf_w[b])
            for tok in self.buf_r.get(b, {}).items():
                add(tok)
        out = []
        wd = self.waited[eng]
        for k, v in deps.items():
            if wd.get(k, 0) < v:
                wd[k] = v
                out.append((k, v))
        return out

    def _commit(self, tok, reads, writes):
        for b in reads:
            d = self.buf_r.setdefault(b, {})
            if d.get(tok[0], 0) < tok[1]:
                d[tok[0]] = tok[1]
        for b in writes:
            self.buf_w[b] = tok
            self.buf_r[b] = {}

    def op(self, eng, fn, reads=(), writes=()):
        waits = self._waits(eng, reads, writes)
        self.eng_cnt[eng] += 1
        tok = (self.eng_sem[eng], self.eng_cnt[eng])
        self.streams[eng].append((waits, fn, (tok[0], 1)))
        self._commit(tok, reads, writes)

    def dma(self, eng, semkey, out, in_, reads=(), writes=()):
        waits = self._waits(eng, reads, writes)
        self.dma_cnt[semkey] = self.dma_cnt.get(semkey, 0) + 16
        tok = (semkey, self.dma_cnt[semkey])
        self.streams[eng].append(
            (waits, (lambda e, o=out, i=in_: e.dma_start(out=o, in_=i)), (semkey, 16)))
        self._commit(tok, reads, writes)

    def final_wait(self, eng, toks):
        self.streams[eng].append((list(toks), None, None))

    def emit(self, eng, e):
        for waits, fn, inc in self.streams[eng]:
            for k, v in waits:
                e.wait_ge(self.sems[k], v)
            if fn is None:
                continue
            ins = fn(e)
            if inc is not None:
                ins.then_inc(self.sems[inc[0]], inc[1])


def build_nc():
    nc = bass.Bass("TRN2", target_bir_lowering=False)
    xblk = nc.dram_tensor("xblk", [NPASS, 128, 8, T], F32, kind="ExternalInput").ap()
    xh = nc.dram_tensor("xh", [128, 8, HL], F32, kind="ExternalInput").ap()
    wblk = nc.dram_tensor("wblk", [NBLK, 128, 4096], F32, kind="ExternalInput").ap()
    cwb_d = nc.dram_tensor("cwb", [128, 8, KB], F32, kind="ExternalInput").ap()
    cwa_d = nc.dram_tensor("cwa", [128, 8, KA], F32, kind="ExternalInput").ap()
    pv_d = nc.dram_tensor("pv", [128, 7, 8], F32, kind="ExternalInput").ap()
    ident_d = nc.dram_tensor("ident", [128, 128], F32, kind="ExternalInput").ap()
    yblk = nc.dram_tensor("yblk", [NPASS, 128, 8, T], F32, kind="ExternalOutput").ap()
    wbf = nc.dram_tensor("wbf", [NBLK, 128, 4096], BF16).ap()

    with ExitStack() as st:
        def sb(name, shape, dt):
            return st.enter_context(nc.sbuf_tensor(name, shape, dt))

        xbf = [sb(f"xbf{i}", [128, 8, T], BF16) for i in range(2)]
        xhbf = sb("xhbf", [128, 8, HL], BF16)
        tA = [sb(f"tA{i}", [128, T], F32) for i in range(4)]
        sgt = [sb(f"sgt{i}", [128, T], F32) for i in range(2)]
        hsm = [sb(f"hsm{i}", [128, HL], F32) for i in range(4)]
        cvb = [sb(f"cvb{i}", [128, T + 2], BF16) for i in range(2)]
        cvh = sb("cvh", [128, 8, 2], BF16)
        diagA = sb("diagA", [128, 8 * KA, 128], BF16)
        diagB = sb("diagB", [128, 8 * NPE, 128], BF16)
        ident = sb("ident_s", [128, 128], F32)
        apre = sb("apre", [128, 8, T], BF16)
        u0 = sb("u0", [128, 8, HL + T], BF16)
        u = sb("u", [128, 8, T], F32)
        sqt = [sb(f"sqt{i}", [128, T], BF16) for i in range(2)]
        bft = [sb(f"bft{i}", [128, T], BF16) for i in range(2)]
        sqc = [sb(f"sqc{i}", [128, T], BF16) for i in range(2)]
        bfc = [sb(f"bfc{i}", [128, T], BF16) for i in range(2)]
        mu = [sb(f"mu{i}", [128, T], F32) for i in range(2)]
        rstd = [sb(f"rstd{i}", [128, T], F32) for i in range(2)]
        tt = [sb(f"tt{i}", [128, T], F32) for i in range(2)]
        bx = sb("bx", [128, 8, T], BF16)
        res = sb("res", [128, 8, T], F32)
        h = sb("h", [128, 32, T], BF16)
        outt = [sb(f"outt{i}", [128, T], F32) for i in range(2)]
        xft = [sb(f"xft{i}", [128, T], F32) for i in range(4)]
        wsl = [sb(f"wsl{i}", [128, 4096], BF16) for i in range(NS)]
        cwb = sb("cwb_s", [128, 8, KB], F32)
        cwa = sb("cwa_s", [128, 8, KA], F32)
        pv = sb("pv_s", [128, 7, 8], F32)
        ones = sb("ones", [128, 128], BF16)
        ps = [st.enter_context(nc.psum_tensor(f"ps{i}", [128, T], F32)) for i in range(8)]

        S = Sched(nc, st)
        wsem = [S.new_sem(f"w{i}") for i in range(NS)]
        castsem = [S.new_sem(f"c{i}") for i in range(NBLK)]
        xbsem = [S.new_sem(f"xb{i}") for i in range(2)]
        xhsem = S.new_sem("xh")
        xfsem = [S.new_sem(f"xf{i}") for i in range(4)]
        osem = [S.new_sem(f"o{i}") for i in range(2)]
        csem = [S.new_sem(f"k{i}") for i in range(4)]

        seq = [B0 + k for k in range(4)] + [A0 + k for k in range(6)]
        for i in range(NPASS):
            if i < NPASS - 1:
                seq += [B0 + k for k in range(4)]
            seq += [S30 + k for k in range(8)] + [WO0, WO0 + 1]
            if i < NPASS - 1:
                seq += [A0 + k for k in range(6)]
            seq += [UP0 + k for k in range(8)] + [DN0 + k for k in range(8)]
        wstate = {"next_load": 0, "use": 0}

        def w_next(expect):
            n = wstate["use"]
            assert seq[n] == expect, (n, seq[n], expect)
            wstate["use"] += 1
            lim = min(n + NS - 1, len(seq) - 1)
            while wstate["next_load"] <= lim:
                L = wstate["next_load"]
                blk, s = seq[L], L % NS
                S.dma("sp", wsem[s], wsl[s][:], wbf[blk],
                      reads=[("wbf", blk)], writes=[("wsl", s)])
                wstate["next_load"] += 1
            return n % NS

        rot = {}

        def rt(name, n):
            v = rot.get(name, 0)
            rot[name] = (v + 1) % n
            return v

        held = set()
        busy = set()
        bank_state = {"n": 0}

        def nb():
            for _ in range(NMAIN):
                b = bank_state["n"]
                bank_state["n"] = (b + 1) % NMAIN
                if b not in held and b not in busy:
                    busy.add(b)
                    return b
            raise RuntimeError("no free PSUM bank")

        def rel(*bs):
            for b in bs:
                busy.discard(b)

        bg = []
        bgacc = {"v": 0.0, "rate": 0.0}

        def drain(n_mm):
            bgacc["v"] += n_mm * bgacc["rate"]
            if bgacc.get("busy"):
                return
            bgacc["busy"] = True
            while bg and bgacc["v"] >= 1.0:
                bgacc["v"] -= 1.0
                bg.pop(0)()
            bgacc["busy"] = False

        def flush_bg():
            bgacc["busy"] = True
            while bg:
                bg.pop(0)()
            bgacc["busy"] = False
            bgacc["v"] = 0.0

        def mm(bank_ap, pairs, reads, writes, start=True, stop=True, nd=None):
            def fn(e, pairs=pairs, bank_ap=bank_ap, start=start, stop=stop):
                last = None
                n = len(pairs)
                for idx, (l, r) in enumerate(pairs):
                    last = e.matmul(bank_ap, lhsT=l, rhs=r,
                                    start=(start and idx == 0), stop=(stop and idx == n - 1))
                return last
            S.op("pe", fn, reads=reads, writes=writes)
            drain(len(pairs) if nd is None else nd)

        def act(out, in_, func, reads, writes, bias=0.0, scale=1.0):
            S.op("act", lambda e: e.activation(out=out, in_=in_, func=func, bias=bias, scale=scale),
                 reads=reads, writes=writes)

        def tten(eng, out, in0, in1, op, reads, writes):
            S.op(eng, lambda e: e.tensor_tensor(out=out, in0=in0, in1=in1, op=op),
                 reads=reads, writes=writes)

        def tsc(eng, out, in0, s1, s2, op0, op1, reads, writes):
            if s2 is None:
                S.op(eng, lambda e: e.tensor_scalar(out=out, in0=in0, scalar1=s1, scalar2=None, op0=op0),
                     reads=reads, writes=writes)
            else:
                S.op(eng, lambda e: e.tensor_scalar(out=out, in0=in0, scalar1=s1, scalar2=s2,
                                                    op0=op0, op1=op1),
                     reads=reads, writes=writes)

        def stt(out, in0, scalar, in1, op0, op1, reads, writes):
            S.op("dve", lambda e: e.scalar_tensor_tensor(out=out, in0=in0, scalar=scalar, in1=in1,
                                                         op0=op0, op1=op1),
                 reads=reads, writes=writes)

        def copy(eng, out, in_, reads, writes):
            S.op(eng, lambda e: e.tensor_copy(out=out, in_=in_), reads=reads, writes=writes)

        def wk(s, kc, q):
            return wsl[s][:, kc * 512 + q * 128: kc * 512 + (q + 1) * 128]

        def stage_F(i, first, pre=None):
            xb = xbf[i % 2]
            s = None
            for j in range(8):
                if pre is not None:
                    ln_flush()
                    ln_fin.append(pre(j))
                if j % 2 == 0:
                    s = w_next(B0 + j // 2)
                q0 = (j % 2) * 2
                bv, bgk = nb(), nb()
                for which, b in ((0, bv), (1, bgk)):
                    mm(ps[b][:, :], [(wk(s, kc, q0 + which), xb[:, kc, :]) for kc in range(8)],
                       reads=[("wsl", s), ("xbf", i % 2)], writes=[("ps", b)])
                r = rt("sgt", 2)
                act(sgt[r][:], ps[bgk][:, :], AF.Tanh, reads=[("ps", bgk)], writes=[("sgt", r)], scale=0.5)
                stt(u0[:, j, HL:HL + T], sgt[r][:], 1.0, ps[bv][:, :], ALU.add, ALU.mult,
                    reads=[("ps", bv), ("sgt", r)], writes=[("u0", j)])
                rel(bv, bgk)
                if first:
                    hv, hg = nb(), nb()
                    for which, b in ((0, hv), (1, hg)):
                        mm(ps[b][:, 0:HL], [(wk(s, kc, q0 + which), xhbf[:, kc, :]) for kc in range(8)],
                           reads=[("wsl", s), "xhbf"], writes=[("ps", b)])
                    rh = rt("hsm", 4)
                    act(hsm[rh][:], ps[hg][:, 0:HL], AF.Tanh, reads=[("ps", hg)], writes=[("hsm", rh)], scale=0.5)
                    stt(u0[:, j, 0:HL], hsm[rh][:], 1.0, ps[hv][:, 0:HL], ALU.add, ALU.mult,
                        reads=[("ps", hv), ("hsm", rh)], writes=[("u0", j)])
                    rel(hv, hg)

        def conv_tasks(i):
            tasks = []

            def peconv(j):
                b = nb()
                mm(ps[b][:, :], [(diagB[:, j * NPE + k, :], u0[:, j, 2 + k:2 + k + T]) for k in range(NPE)],
                   reads=[("u0", j), "diagB"], writes=[("ps", b)], nd=0)
                act(u[:, j, :], ps[b][:, :], AF.Identity, reads=[("ps", b), "cst"], writes=[("u", j)],
                    bias=pv[:, 0, j:j + 1])
                rel(b)

            def tap(j, k):
                stt(u[:, j, :], u0[:, j, 2 + k:2 + k + T], cwb[:, j, k:k + 1], u[:, j, :],
                    ALU.mult, ALU.add, reads=[("u0", j), ("u", j), "cst"], writes=[("u", j)])

            def stats(j):
                cell = {}
                def t1():
                    r = cell["r"] = rt("sqc", 2)
                    act(sqc[r][:], u[:, j, :], AF.Square, reads=[("u", j)], writes=[("sqc", r)])
                def t2():
                    r = cell["r"]
                    act(bfc[r][:], u[:, j, :], AF.Copy, reads=[("u", j)], writes=[("bfc", r)])
                def t3():
                    r = cell["r"]
                    mm(ps[6][:, :], [(ones[:, :], bfc[r][:])], reads=[("bfc", r), "ones"],
                       writes=[("ps", 6)], start=(j == 0), stop=(j == 7), nd=0)
                def t4():
                    r = cell["r"]
                    mm(ps[7][:, :], [(ones[:, :], sqc[r][:])], reads=[("sqc", r), "ones"],
                       writes=[("ps", 7)], start=(j == 0), stop=(j == 7), nd=0)
                return [t1, t2, t3, t4]

            late = []
            for j in range(4):
                tasks.append(lambda j=j: peconv(j))
            for jp in range(4):
                ja, jb = 2 * jp, 2 * jp + 1
                for k in range(NPE, KB):
                    tasks.append(lambda j=ja, k=k: tap(j, k))
                    tasks.append(lambda j=jb, k=k: tap(j, k))
                    if k == NPE + 5:
                        tasks += late
                        late = []
                        if jp < 2:
                            tasks.append(lambda j=2 * jp + 4: peconv(j))
                            tasks.append(lambda j=2 * jp + 5: peconv(j))
                sa_, sb_ = stats(ja), stats(jb)
                tasks += sa_[:2] + sb_[:2]
                late = sa_[2:] + sb_[2:]
            tasks += late
            return tasks

        def ln_finalize(bm, be, li):
            act(mu[li][:], ps[bm][:, :], AF.Copy, reads=[("ps", bm)], writes=[("mu", li)])
            act(rstd[li][:], ps[bm][:, :], AF.Square, reads=[("ps", bm)], writes=[("rstd", li)])
            stt(rstd[li][:], ps[be][:, :], EPS, rstd[li][:], ALU.add, ALU.subtract,
                reads=[("ps", be), ("rstd", li)], writes=[("rstd", li)])
            act(rstd[li][:], rstd[li][:], AF.Ln, reads=[("rstd", li)], writes=[("rstd", li)])
            act(rstd[li][:], rstd[li][:], AF.Exp, reads=[("rstd", li)], writes=[("rstd", li)], scale=-0.5)

        def ln_center(eng, src, srcid, li):
            r = rt("tt", 2)
            tten(eng, tt[r][:], src, mu[li][:], ALU.subtract,
                 reads=[srcid, ("mu", li)], writes=[("tt", r)])
            tten(eng, tt[r][:], tt[r][:], rstd[li][:], ALU.mult,
                 reads=[("tt", r), ("rstd", li)], writes=[("tt", r)])
            return r

        def lnc_apply(j):
            r = ln_center("pool", u[:, j, :], ("u", j), 0)
            def fin():
                act(bx[:, j, :], tt[r][:], AF.Silu, reads=[("tt", r), "cst"], writes=[("bx", j)],
                    bias=pv[:, 2, j:j + 1], scale=pv[:, 1, j:j + 1])
            return fin

        def ln1_apply(e_):
            r = ln_center("pool", res[:, e_, :], ("res", e_), 0)
            def fin():
                act(res[:, e_, :], tt[r][:], AF.Identity, reads=[("tt", r), "cst"], writes=[("res", e_)],
                    bias=pv[:, 4, e_:e_ + 1], scale=pv[:, 3, e_:e_ + 1])
                act(bx[:, e_, :], tt[r][:], AF.Identity, reads=[("tt", r), "cst"], writes=[("bx", e_)],
                    bias=pv[:, 4, e_:e_ + 1], scale=pv[:, 3, e_:e_ + 1])
            return fin

        def ln2_apply(i, e_, eng="pool"):
            r = ln_center(eng, res[:, e_, :], ("res", e_), 1)
            def fin():
                ro = rt("outt", 2)
                act(outt[ro][:], tt[r][:], AF.Identity, reads=[("tt", r), "cst"], writes=[("outt", ro)],
                    bias=pv[:, 6, e_:e_ + 1], scale=pv[:, 5, e_:e_ + 1])
                S.dma("act", osem[ro], yblk[i, :, e_, :], outt[ro][:],
                      reads=[("outt", ro)], writes=[("y", i, e_)])
            return fin

        ln_fin = []

        def ln_flush(keep=0):
            while len(ln_fin) > keep:
                ln_fin.pop(0)()

        s1a_pend = []

        def stage_S1A(i, first, j):
            xb = xbf[i % 2]
            ra = rt("tA", 4)
            rb = rt("sgt", 2)
            rc = rt("cvb", 2)
            c = cvb[rc]
            cid = ("cvb", rc)
            h0 = None
            for w3 in range(3):
                q = 3 * j + w3
                if q % 4 == 0:
                    stage_S1A.s = w_next(A0 + q // 4)
                s = stage_S1A.s
                b = nb()
                mm(ps[b][:, :], [(wk(s, kc, q % 4), xb[:, kc, :]) for kc in range(8)],
                   reads=[("wsl", s), ("xbf", i % 2)], writes=[("ps", b)])
                b2 = None
                if first and w3 < 2:
                    b2 = nb()
                    mm(ps[b2][:, 0:HL], [(wk(s, kc, q % 4), xhbf[:, kc, :]) for kc in range(8)],
                       reads=[("wsl", s), "xhbf"], writes=[("ps", b2)])
                if w3 == 0:
                    act(tA[ra][:], ps[b][:, :], AF.Copy, reads=[("ps", b)], writes=[("tA", ra)])
                    rel(b)
                    if first:
                        h0 = rt("hsm", 4)
                        act(hsm[h0][:], ps[b2][:, 0:HL], AF.Copy, reads=[("ps", b2)], writes=[("hsm", h0)])
                        rel(b2)
                elif w3 == 1:
                    if first:
                        tten("dve", cvh[:, j, :], ps[b2][:, HL - 2:HL], hsm[h0][:, HL - 2:HL], ALU.mult,
                             reads=[("ps", b2), ("hsm", h0)], writes=[("cvh", j)])
                        rel(b2)
                    copy("dve", c[:, 0:2], cvh[:, j, :], reads=[("cvh", j)], writes=[cid])
                    tten("dve", c[:, 2:2 + T], ps[b][:, :], tA[ra][:], ALU.mult,
                         reads=[("ps", b), ("tA", ra)], writes=[cid])
                    rel(b)
                    copy("dve", cvh[:, j, :], c[:, T:T + 2], reads=[cid], writes=[("cvh", j)])
                else:
                    act(sgt[rb][:], ps[b][:, :], AF.Copy, reads=[("ps", b)], writes=[("sgt", rb)])
                    rel(b)
            while s1a_pend:
                s1a_pend.pop(0)()

            def tail(j=j, c=c, cid=cid, rb=rb):
                bcv = nb()
                mm(ps[bcv][:, :], [(diagA[:, j * KA + k, :], c[:, k:k + T]) for k in range(KA)],
                   reads=[cid, "diagA"], writes=[("ps", bcv)])
                tten("dve", apre[:, j, :], ps[bcv][:, :], sgt[rb][:], ALU.mult,
                     reads=[("ps", bcv), ("sgt", rb)], writes=[("apre", j)])
                rel(bcv)
            s1a_pend.append(tail)

        def stage_S3(i, extra):
            xb = xbf[i % 2]
            for j in range(8):
                s = w_next(S30 + j)
                rr = []
                for half, (rhs_t, rid) in enumerate(((apre, "apre"), (bx, "bx"))):
                    bgte, by = nb(), nb()
                    mm(ps[bgte][:, :], [(wk(s, kc, 2 * half), xb[:, kc, :]) for kc in range(8)],
                       reads=[("wsl", s), ("xbf", i % 2)], writes=[("ps", bgte)])
                    r = rt("tA", 4)
                    rr.append(r)
                    act(tA[r][:], ps[bgte][:, :], AF.Sigmoid, reads=[("ps", bgte)], writes=[("tA", r)])
                    rel(bgte)
                    mm(ps[by][:, :], [(wk(s, kc, 2 * half + 1), rhs_t[:, kc, :]) for kc in range(8)],
                       reads=[("wsl", s)] + [(rid, kc) for kc in range(8)], writes=[("ps", by)])
                    tten("dve", tA[r][:], ps[by][:, :], tA[r][:], ALU.mult,
                         reads=[("ps", by), ("tA", r)], writes=[("tA", r)])
                    rel(by)
                tten("dve" if j >= 6 else "pool", h[:, j, :], tA[rr[0]][:], tA[rr[1]][:], ALU.add,
                     reads=[("tA", rr[0]), ("tA", rr[1])], writes=[("h", j)])
                ln_flush()
                for _ in range(2):
                    if extra and j < 6:
                        ln_flush(1)
                        ln_fin.append(extra.pop(0)())
            ln_flush()
            assert not extra

        def stats_prep(e_, srcid):
            r = rt("sq", 2)
            act(sqt[r][:], res[:, e_, :], AF.Square, reads=[srcid], writes=[("sqt", r)])
            act(bft[r][:], res[:, e_, :], AF.Copy, reads=[srcid], writes=[("bft", r)])
            return r

        def stats_mm(e_, r, bm, be):
            mm(ps[bm][:, :], [(ones[:, :], bft[r][:])], reads=[("bft", r), "ones"],
               writes=[("ps", bm)], start=(e_ == 0), stop=(e_ == 7))
            mm(ps[be][:, :], [(ones[:, :], sqt[r][:])], reads=[("sqt", r), "ones"],
               writes=[("ps", be)], start=(e_ == 0), stop=(e_ == 7))

        def xload(i, e_):
            S.dma("pool", xfsem[e_ % 4], xft[e_ % 4][:], xblk[i, :, e_, :], reads=[],
                  writes=[("xft", e_ % 4)])

        def stage_mix(i):
            bm, be = nb(), nb()
            held.update((bm, be))
            pend = None
            s = w_next(WO0)
            pre = {}
            for e_ in range(3):
                pre[e_] = nb()
                mm(ps[pre[e_]][:, :], [(wk(s, kc, e_), h[:, kc, :]) for kc in range(7)],
                   reads=[("wsl", s)] + [("h", kc) for kc in range(7)], writes=[("ps", pre[e_])],
                   start=True, stop=False)
            for e_ in range(8):
                if e_ == 4:
                    s = w_next(WO0 + 1)
                if e_ in pre:
                    b = pre[e_]
                    mm(ps[b][:, :], [(wk(s, 7, e_ % 4), h[:, 7, :])],
                       reads=[("wsl", s), ("h", 7)], writes=[("ps", b)], start=False, stop=True)
                else:
                    b = nb()
                    mm(ps[b][:, :], [(wk(s, kc, e_ % 4), h[:, kc, :]) for kc in range(8)],
                       reads=[("wsl", s)] + [("h", kc) for kc in range(8)], writes=[("ps", b)])
                rx = e_ % 4
                if pend is not None:
                    stats_mm(*pend)
                stt(res[:, e_, :], xft[rx][:], ALPHA, ps[b][:, :], ALU.mult, ALU.add,
                    reads=[("xft", rx), ("ps", b)], writes=[("res", e_)])
                rel(b)
                if e_ + 4 < 8:
                    xload(i, e_ + 4)
                pend = (e_, stats_prep(e_, ("res", e_)), bm, be)
            stats_mm(*pend)
            ln_finalize(bm, be, 0)
            held.difference_update((bm, be))
            rel(bm, be)
            bank_state["n"] = (be + 1) % NMAIN

        def stage_up(i):
            s = w_next(UP0)
            pre = {}
            for f in range(3):
                pre[f] = nb()
                mm(ps[pre[f]][:, :], [(wk(s, kc, f), bx[:, kc, :]) for kc in range(7)],
                   reads=[("wsl", s)] + [("bx", kc) for kc in range(7)], writes=[("ps", pre[f])],
                   start=True, stop=False)
            for f in range(32):
                if f % 4 == 0 and f > 0:
                    s = w_next(UP0 + f // 4)
                if f in pre:
                    b = pre[f]
                    mm(ps[b][:, :], [(wk(s, 7, f % 4), bx[:, 7, :])],
                       reads=[("wsl", s), ("bx", 7)], writes=[("ps", b)], start=False, stop=True)
                else:
                    b = nb()
                    mm(ps[b][:, :], [(wk(s, kc, f % 4), bx[:, kc, :]) for kc in range(8)],
                       reads=[("wsl", s)] + [("bx", kc) for kc in range(8)], writes=[("ps", b)])
                r = rt("relu", 2)
                act(tA[r][:], ps[b][:, :], AF.Relu, reads=[("ps", b)], writes=[("tA", r)])
                rel(b)
                act(h[:, f, :], tA[r][:], AF.Square, reads=[("tA", r)], writes=[("h", f)])

        def stage_down(i):
            bm, be = nb(), nb()
            held.update((bm, be))
            pend = None
            for e_ in range(8):
                s = w_next(DN0 + e_)
                b = nb()
                dpairs = [(wsl[s][:, kc * 128:(kc + 1) * 128], h[:, kc, :]) for kc in range(32)]
                if e_ == 0:
                    mm(ps[b][:, :], dpairs[:26], reads=[("wsl", s)] + [("h", kc) for kc in range(26)],
                       writes=[("ps", b)], start=True, stop=False)
                    mm(ps[b][:, :], dpairs[26:], reads=[("wsl", s)] + [("h", kc) for kc in range(26, 32)],
                       writes=[("ps", b)], start=False, stop=True)
                else:
                    mm(ps[b][:, :], dpairs, reads=[("wsl", s)] + [("h", kc) for kc in range(32)],
                       writes=[("ps", b)])
                if pend is not None:
                    stats_mm(*pend)
                stt(res[:, e_, :], res[:, e_, :], ALPHA, ps[b][:, :], ALU.mult, ALU.add,
                    reads=[("res", e_), ("ps", b)], writes=[("res", e_)])
                rel(b)
                pend = (e_, stats_prep(e_, ("res", e_)), bm, be)
            stats_mm(*pend)
            ln_finalize(bm, be, 1)
            held.difference_update((bm, be))
            rel(bm, be)
            bank_state["n"] = (be + 1) % NMAIN

        S.dma("sp", csem[0], cwb[:], cwb_d, writes=["cst"])
        S.dma("sp", csem[1], cwa[:], cwa_d, writes=["cst"])
        S.op("dve", lambda e: e.tensor_scalar(out=cwb[:], in0=cwb[:], scalar1=0.5, scalar2=None, op0=ALU.mult),
             reads=["cst"], writes=["cst"])
        S.dma("sp", csem[2], pv[:], pv_d, writes=["cst"])
        S.dma("sp", csem[3], ident[:], ident_d, writes=["ident"])
        for j in range(8):
            for k in range(KA):
                act(diagA[:, j * KA + k, :], ident[:], AF.Copy, reads=["ident", "cst"], writes=["diagA"],
                    scale=cwa[:, j, k:k + 1])
        for j in range(8):
            for k in range(NPE):
                act(diagB[:, j * NPE + k, :], ident[:], AF.Copy, reads=["ident", "cst"], writes=["diagB"],
                    scale=cwb[:, j, k:k + 1])
        S.op("pool", lambda e: e.memset(ones[:], 1.0 / D), writes=["ones"])
        S.dma("pool", xhsem, xhbf[:], xh, writes=["xhbf"])
        S.dma("pool", xbsem[0], xbf[0][:], xblk[0], writes=[("xbf", 0)])
        first_use = []
        for b in seq:
            if b not in first_use:
                first_use.append(b)

        def casts(blks):
            for blk in blks:
                S.dma("pool", castsem[blk], wbf[blk], wblk[blk], writes=[("wbf", blk)])

        def pool_wait_loads():
            S.final_wait("pool", [(wsem[k], S.dma_cnt[wsem[k]]) for k in range(NS) if wsem[k] in S.dma_cnt])

        casts(first_use[0:1])
        S.final_wait("pool", [(castsem[first_use[0]], 16)])
        casts(first_use[1:4])
        S.dma("pool", xbsem[1], xbf[1][:], xblk[1], writes=[("xbf", 1)])
        S.final_wait("pool", [(wsem[0], 16)])
        casts(first_use[4:10])

        stage_F(0, True)
        pool_wait_loads()
        casts(first_use[10:20])
        bg.extend(conv_tasks(0))
        bgacc["rate"] = len(bg) / 400.0
        for j in range(8):
            stage_S1A(0, True, j)
            if j == 3:
                pool_wait_loads()
                casts(first_use[20:])
        while s1a_pend:
            s1a_pend.pop(0)()
        flush_bg()

        pending_ln2 = None
        for i in range(NPASS):
            last = i == NPASS - 1
            flush_bg()
            ln_finalize(6, 7, 0)
            ln2_tasks = []
            if pending_ln2 is not None:
                ln2_tasks = [(lambda p=pending_ln2, e_=e_: ln2_apply(p, e_)) for e_ in range(8)]
                pending_ln2 = None
            if last:
                for j in range(8):
                    ln_flush()
                    ln_fin.append(lnc_apply(j))
            ln_flush()
            if not last:
                S.op("dve", lambda e: e.tensor_copy(out=u0[:, :, 0:HL], in_=u0[:, :, T:T + HL]),
                     reads=[("u0", j) for j in range(8)], writes=[("u0", j) for j in range(8)])
                bgacc["rate"] = 0.0
                stage_F(i + 1, False, pre=lnc_apply)
                ln_flush()
                bg.extend(conv_tasks(i + 1))
            nbg = len(bg)
            bgacc["rate"] = 0.22 * nbg / 280.0
            for e_ in range(4):
                xload(i, e_)
            stage_S3(i, ln2_tasks)
            if i + 2 < NPASS:
                S.dma("pool", xbsem[i % 2], xbf[i % 2][:], xblk[i + 2], writes=[("xbf", i % 2)])
            stage_mix(i)
            bgacc["rate"] = 0.20 * nbg / 280.0
            for j in range(8):
                if not last:
                    stage_S1A(i + 1, False, j)
                ln_flush()
                if j < 4:
                    ln_fin.append(ln1_apply(2 * j))
                    ln_fin.append(ln1_apply(2 * j + 1))
            ln_flush()
            while s1a_pend:
                s1a_pend.pop(0)()
            bgacc["rate"] = 0.37 * nbg / 280.0
            stage_up(i)
            bgacc["rate"] = 0.36 * nbg / 280.0
            stage_down(i)
            pending_ln2 = i
        for e_ in range(8):
            ln_flush(1)
            ln_fin.append(ln2_apply(pending_ln2, e_, "pool" if e_ % 4 == 1 else "dve"))
        ln_flush()
        S.final_wait("act", [(k, S.dma_cnt[k]) for k in osem])

        with nc.Block() as block:
            @block.tensor
            def _(e):
                S.emit("pe", e)

            @block.scalar
            def _(e):
                S.emit("act", e)

            @block.vector
            def _(e):
                S.emit("dve", e)

            @block.gpsimd
            def _(e):
                S.emit("pool", e)

            @block.sync
            def _(e):
                S.emit("sp", e)
    return nc


def _weight_blocks(w_in, w_out_a, w_out_b, w_o, w_up, w_down):
    blocks = np.empty((NBLK, 128, 4096), np.float32)

    def kblock(cols_src):
        return cols_src.reshape(8, 128, 512).transpose(1, 0, 2).reshape(128, 4096)

    cA, cC, cV, cVal, cGate, cGa, cGb = [k * D for k in range(7)]
    oB, oC, oV, oVal, oGate, oGa, oGb = cA, cC, cV, cVal, cGate, cGa, cGb

    def col(off, j):
        return w_in[:, off + j * 128: off + (j + 1) * 128]

    for m in range(4):
        cols = np.concatenate([col(oVal, 2 * m), col(oGate, 2 * m),
                               col(oVal, 2 * m + 1), col(oGate, 2 * m + 1)], axis=1)
        blocks[B0 + m] = kblock(cols)
    for j in range(8):
        cols = np.concatenate([col(oGa, j), w_out_a[:, j * 128:(j + 1) * 128],
                               col(oGb, j), w_out_b[:, j * 128:(j + 1) * 128]], axis=1)
        blocks[S30 + j] = kblock(cols)
    for m in range(2):
        blocks[WO0 + m] = kblock(w_o[:, m * 512:(m + 1) * 512])
    a_cols = []
    for j in range(8):
        a_cols += [col(oC, j), col(oV, j), col(oB, j)]
    for m in range(6):
        blocks[A0 + m] = kblock(np.concatenate(a_cols[4 * m:4 * m + 4], axis=1))
    for m in range(8):
        blocks[UP0 + m] = kblock(w_up[:, m * 512:(m + 1) * 512])
    for e_ in range(8):
        blocks[DN0 + e_] = (w_down[:, e_ * 128:(e_ + 1) * 128]
                            .reshape(32, 128, 128).transpose(1, 0, 2).reshape(128, 4096))
    return blocks


_NC_CACHE = {}


def kernel(x, w_in, conv_a_w, w_out_a, conv_b_w, conv_b_bias, ln_b_gamma, ln_b_beta,
           w_out_b, w_o, ln1_gamma, ln1_beta, w_up, w_down, ln2_gamma, ln2_beta):
    f = lambda a: np.ascontiguousarray(np.asarray(a, dtype=np.float32))
    x = f(x)
    wblk = _weight_blocks(f(w_in), f(w_out_a), f(w_out_b), f(w_o), f(w_up), f(w_down))
    cwb = np.ascontiguousarray(f(conv_b_w).T.reshape(8, 128, KB).transpose(1, 0, 2))
    cwa = np.ascontiguousarray(f(conv_a_w).T.reshape(8, 128, KA).transpose(1, 0, 2))
    vecs = [conv_b_bias, ln_b_gamma, ln_b_beta, ln1_gamma, ln1_beta, ln2_gamma, ln2_beta]
    pv = np.ascontiguousarray(np.stack([f(v).reshape(8, 128).T for v in vecs], axis=1))

    in_maps = []
    for c in range(NCORES):
        b, half = c // 2, c % 2
        t0 = half * TOK
        xs = x[b, t0:t0 + TOK, :]
        xb = np.ascontiguousarray(xs.reshape(NPASS, T, 8, 128).transpose(0, 3, 2, 1))
        if half == 0:
            xh = np.zeros((128, 8, HL), np.float32)
        else:
            xh = np.ascontiguousarray(x[b, t0 - HL:t0, :].reshape(HL, 8, 128).transpose(2, 1, 0))
        in_maps.append({"xblk": xb, "xh": xh, "wblk": wblk, "cwb": cwb, "cwa": cwa, "pv": pv,
                        "ident": np.eye(128, dtype=np.float32)})

    if "nc" not in _NC_CACHE:
        _NC_CACHE["nc"] = build_nc()
    nc = _NC_CACHE["nc"]
    res = run_bass_kernel_spmd(nc, in_maps, core_ids=list(range(NCORES)))
    out = np.empty((BATCH, SEQ, D), np.float32)
    for c in range(NCORES):
        b, half = c // 2, c % 2
        y = np.asarray(res.results[c]["yblk"], dtype=np.float32)
        out[b, half * TOK:(half + 1) * TOK, :] = y.transpose(0, 3, 2, 1).reshape(TOK, D)
    return out
```
